# Optimizing a Trainium2 kernel written in Bass

```python
import math, functools
import jax, jax.numpy as jnp
from jax import lax
import numpy as np

D_MODEL = 1024
BATCH = 8
SEQ = 2048
DEPTH = 2

GRID_W = 64
CTX_LEN = 256
D_MIX = D_MODEL
CONV_W = 4
CHUNK = 64
EPS = 1e-6
DA = D_MIX // 4
NH_A = 4
DH_A = DA // NH_A
DB = D_MIX // 2
HEAD_B = 64
NH_B = DB // HEAD_B
NG_B = 2
HPG_B = NH_B // NG_B
D_STATE = 64
DC = D_MIX - DA - DB
NB_C = 4
BW_C = DC // NB_C
RG_C = 8.0
D_FF = 4 * D_MODEL
SPLITS = (DA, DA, DA, DA, 2 * NH_A, 2 * NH_A, DB, DB, NG_B * D_STATE, NG_B * D_STATE, 2 * NH_B, DC, DC)
P_IN = sum(SPLITS)

kernel_name = "hybrid_mlstm_ssd_rglru_prefix_dit_block"


def rmsnorm(x, g):
    xf = x.astype(jnp.float32)
    y = xf * lax.rsqrt(jnp.mean(xf * xf, axis=-1, keepdims=True) + EPS)
    return (y * g.astype(jnp.float32)).astype(x.dtype)


def centred_conv(x, w, b):
    L = x.shape[1]
    left = CONV_W // 2
    xp = jnp.pad(x, ((0, 0), (left, CONV_W - 1 - left), (0, 0)))
    return b + sum(w[j] * xp[:, j:j + L] for j in range(CONV_W))


def split_proj(p):
    out, idx = [], 0
    for s in SPLITS:
        out.append(p[..., idx:idx + s])
        idx += s
    return out


def to_chunks(t):
    b, l = t.shape[:2]
    return jnp.moveaxis(t.reshape(b, l // CHUNK, CHUNK, *t.shape[2:]), 1, 0)


def from_chunks(t):
    t = jnp.moveaxis(t, 0, 1)
    return t.reshape(t.shape[0], -1, *t.shape[3:])


def to_column_major(t, rows):
    b, l, ch = t.shape
    return t.reshape(b, rows, GRID_W, ch).transpose(0, 2, 1, 3).reshape(b, l, ch)


def from_column_major(t, rows):
    b, l, ch = t.shape
    return t.reshape(b, GRID_W, rows, ch).transpose(0, 2, 1, 3).reshape(b, l, ch)


def bidirectional(scan_fns, ctx_in, lat_in, zero_state):
    y_ctx, y_lat = 0.0, 0.0
    for d in range(2):
        order = (lambda t: t) if d == 0 else (lambda t: jnp.flip(t, axis=1))
        yc, state = scan_fns[d](*[order(t) for t in ctx_in[d]], state=zero_state)
        yl, _ = scan_fns[d](*[order(t) for t in lat_in[d]], state=state)
        y_ctx = y_ctx + order(yc)
        y_lat = y_lat + order(yl)
    return y_ctx, y_lat


def mlstm_scan(q, k, v, ig, fg, state):
    tril = jnp.tril(jnp.ones((CHUNK, CHUNK), bool))[None, :, :, None]
    logf = jax.nn.log_sigmoid(fg)

    def step(carry, inp):
        C, n, m = carry
        qc, kc, vc, ic, lf = inp
        b = jnp.cumsum(lf, axis=1)
        dmat = jnp.where(tril, b[:, :, None] - b[:, None] + ic[:, None], -jnp.inf)
        inter = b + m[:, None]
        m_comb = jnp.maximum(inter, jnp.max(dmat, axis=2))
        s = jnp.einsum("bthd,bshd->btsh", qc, kc) * jnp.exp(dmat - m_comb[:, :, None])
        w_inter = jnp.exp(inter - m_comb)
        num = jnp.einsum("btsh,bshd->bthd", s, vc) + w_inter[..., None] * jnp.einsum("bhvk,bthk->bthv", C, qc)
        den = jnp.sum(s, axis=2) + w_inter * jnp.einsum("bhk,bthk->bth", n, qc)
        h = num / jnp.maximum(jnp.abs(den), jnp.exp(-m_comb))[..., None]
        b_end = b[:, -1]
        log_w = b_end[:, None] - b + ic
        m_new = jnp.maximum(b_end + m, jnp.max(log_w, axis=1))
        w = jnp.exp(log_w - m_new[:, None])
        decay = jnp.exp(b_end + m - m_new)
        C = decay[..., None, None] * C + jnp.einsum("bth,bthv,bthk->bhvk", w, vc, kc)
        n = decay[..., None] * n + jnp.einsum("bth,bthk->bhk", w, kc)
        return (C, n, m_new), h

    state, h = lax.scan(step, state, tuple(to_chunks(t) for t in (q, k, v, ig, logf)))
    return from_chunks(h), state


def ssd_scan(A, x, dt, Bm, Cm, state):
    tril = jnp.tril(jnp.ones((CHUNK, CHUNK), bool))[None, :, :, None, None]

    def step(h, inp):
        xc, dtc, bc, cc = inp
        cs = jnp.cumsum(dtc * A, axis=1)
        seg = jnp.exp(jnp.where(tril, cs[:, :, None] - cs[:, None], -jnp.inf))
        w = jnp.einsum("btgn,bsgn->btsg", cc, bc)[..., None] * seg * dtc[:, None]
        y = (jnp.einsum("btsge,bsgep->btgep", w, xc)
             + jnp.exp(cs)[..., None] * jnp.einsum("btgn,bgepn->btgep", cc, h))
        c_end = cs[:, -1]
        ws = jnp.exp(c_end[:, None] - cs) * dtc
        h = jnp.exp(c_end)[..., None, None] * h + jnp.einsum("bsge,bsgep,bsgn->bgepn", ws, xc, bc)
        return h, y

    state, y = lax.scan(step, state, tuple(to_chunks(t) for t in (x, dt, Bm, Cm)))
    return from_chunks(y), state


def rglru_scan(w_rg, b_rg, lam, x, state):
    bsz, L = x.shape[:2]
    gates = jnp.einsum("blnc,gncd->gblnd", x.reshape(bsz, L, NB_C, BW_C), w_rg).reshape(2, bsz, L, DC)
    gates = gates + b_rg[:, None, None]
    r, i = jax.nn.sigmoid(gates[0]), jax.nn.sigmoid(gates[1])
    log_a = -RG_C * r * jax.nn.softplus(-lam)
    a = jnp.exp(log_a)
    u = jnp.sqrt(-jnp.expm1(2.0 * log_a)) * (i * x)

    def combine(e1, e2):
        return e1[0] * e2[0], e2[0] * e1[1] + e2[1]

    a_cum, u_cum = lax.associative_scan(combine, (a, u), axis=1)
    h = a_cum * state[:, None] + u_cum
    return h, h[:, -1]


def mlstm_mixer(ctx_parts, lat_parts, conv_w, conv_b, b_ig, b_fg, g_head):
    def prep(q, k, v, o, ig, fg):
        bsz, L = q.shape[:2]
        qk = jax.nn.silu(centred_conv(jnp.concatenate([q, k], axis=-1), conv_w, conv_b))
        qh = qk[..., :DA].reshape(bsz, L, NH_A, DH_A) * (DH_A ** -0.5)
        kh = qk[..., DA:].reshape(bsz, L, NH_A, DH_A)
        vh = v.reshape(bsz, L, NH_A, DH_A)
        ig = ig.reshape(bsz, L, 2, NH_A) + b_ig
        fg = fg.reshape(bsz, L, 2, NH_A) + b_fg
        return [(qh, kh, vh, ig[:, :, d], fg[:, :, d]) for d in range(2)], o

    ctx_in, o_c = prep(*ctx_parts)
    lat_in, o_l = prep(*lat_parts)
    bsz = o_c.shape[0]
    zero = (jnp.zeros((bsz, NH_A, DH_A, DH_A), jnp.float32),
            jnp.zeros((bsz, NH_A, DH_A), jnp.float32),
            jnp.zeros((bsz, NH_A), jnp.float32))
    h_c, h_l = bidirectional([mlstm_scan, mlstm_scan], ctx_in, lat_in, zero)

    def out(h, o):
        return rmsnorm(h, g_head).reshape(o.shape) * jax.nn.sigmoid(o)

    return out(h_c, o_c), out(h_l, o_l)


def ssd_mixer(ctx_parts, lat_parts, conv_w, conv_b, dt_bias, a_log, d_skip, g_norm):
    A = -jnp.exp(a_log).reshape(2, NG_B, HPG_B)

    def prep(z, xs, bs, cs, dt):
        bsz, L = z.shape[:2]
        xbc = jax.nn.silu(centred_conv(jnp.concatenate([xs, bs, cs], axis=-1), conv_w, conv_b))
        xh = xbc[..., :DB].reshape(bsz, L, NG_B, HPG_B, HEAD_B)
        bm = xbc[..., DB:DB + NG_B * D_STATE].reshape(bsz, L, NG_B, D_STATE)
        cm = xbc[..., DB + NG_B * D_STATE:].reshape(bsz, L, NG_B, D_STATE)
        dt = jax.nn.softplus(dt.reshape(bsz, L, 2, NG_B, HPG_B) + dt_bias.reshape(2, NG_B, HPG_B))
        return [(xh, dt[:, :, d], bm, cm) for d in range(2)], (xh, z)

    ctx_in, e_c = prep(*ctx_parts)
    lat_in, e_l = prep(*lat_parts)
    bsz = e_c[1].shape[0]
    zero = jnp.zeros((bsz, NG_B, HPG_B, HEAD_B, D_STATE), jnp.float32)
    scans = [functools.partial(ssd_scan, A[d]) for d in range(2)]
    y_c, y_l = bidirectional(scans, ctx_in, lat_in, zero)

    def out(y, extra):
        xh, z = extra
        y = y + d_skip.reshape(NG_B, HPG_B, 1) * xh
        return rmsnorm(y.reshape(z.shape) * jax.nn.silu(z), g_norm)

    return out(y_c, e_c), out(y_l, e_l)


def rglru_mixer(ctx_parts, lat_parts, rows, conv_w, conv_b, w_rg, b_rg, lam):
    xr_c, gr_c = ctx_parts
    xr_l, gr_l = lat_parts
    xc = centred_conv(xr_c, conv_w, conv_b)
    xl = centred_conv(to_column_major(xr_l, rows), conv_w, conv_b)
    zero = jnp.zeros((xc.shape[0], DC), jnp.float32)
    scans = [functools.partial(rglru_scan, w_rg[d], b_rg[d], lam[d]) for d in range(2)]
    h_c, h_l = bidirectional(scans, [(xc,), (xc,)], [(xl,), (xl,)], zero)
    return h_c * jax.nn.gelu(gr_c), from_column_major(h_l, rows) * jax.nn.gelu(gr_l)


def token_mixers(p_ctx, p_lat, rows, conv_a_w, conv_a_b, b_ig, b_fg, g_head_a, conv_b_w, conv_b_b,
                 dt_bias, a_log, d_skip, g_norm_b, conv_c_w, conv_c_b, w_rg, b_rg, lam):
    pc = split_proj(p_ctx.astype(jnp.float32))
    pl = split_proj(p_lat.astype(jnp.float32))
    a_c, a_l = mlstm_mixer(pc[0:6], pl[0:6], conv_a_w, conv_a_b, b_ig, b_fg, g_head_a)
    b_c, b_l = ssd_mixer(pc[6:11], pl[6:11], conv_b_w, conv_b_b, dt_bias, a_log, d_skip, g_norm_b)
    c_c, c_l = rglru_mixer(pc[11:13], pl[11:13], rows, conv_c_w, conv_c_b, w_rg, b_rg, lam)
    return jnp.concatenate([a_c, b_c, c_c], axis=-1), jnp.concatenate([a_l, b_l, c_l], axis=-1)


def modulation(cvec, w_ada, b_ada):
    m = jax.nn.silu(cvec) @ w_ada + b_ada
    return jnp.split(m[:, None, :], 6, axis=-1)


def modulate(h, shift, scale):
    return h * (1.0 + scale) + shift


def sq_relu_mlp(h, w1, b1, w2, b2):
    return jnp.square(jax.nn.relu(h @ w1 + b1)) @ w2 + b2


def setup_inputs(seed: int = 0) -> dict:
    key = jax.random.key(seed)
    keys = iter(jax.random.split(key, 48))

    def nrm(shape, scale):
        return scale * jax.random.normal(next(keys), shape, jnp.float32)

    def unif(shape, lo, hi):
        return jax.random.uniform(next(keys), shape, jnp.float32, lo, hi)

    def gain(shape):
        return 1.0 + nrm(shape, 0.02)

    n_xbc = DB + 2 * NG_B * D_STATE
    dt0 = jnp.exp(unif((DEPTH, 2, NH_B), math.log(1e-3), math.log(1e-1)))
    log_a0 = jnp.log(unif((DEPTH, 2, DC), 0.9, 0.999)) / RG_C
    return {
        "x": nrm((BATCH, SEQ, D_MODEL), 1.0),
        "c": nrm((BATCH, D_MODEL), 1.0),
        "ctx": nrm((BATCH, CTX_LEN, D_MODEL), 1.0),
        "c_ctx": nrm((D_MODEL,), 1.0),
        "w_ada": nrm((DEPTH, D_MODEL, 6 * D_MODEL), 0.5 * D_MODEL ** -0.5),
        "b_ada": nrm((DEPTH, 6 * D_MODEL), 0.02),
        "g_mix": gain((DEPTH, D_MODEL)),
        "w_in": nrm((DEPTH, D_MODEL, P_IN), D_MODEL ** -0.5),
        "conv_a_w": nrm((DEPTH, CONV_W, 2 * DA), 0.5),
        "conv_a_b": nrm((DEPTH, 2 * DA), 0.02),
        "b_ig": nrm((DEPTH, 2, NH_A), 0.1),
        "b_fg": jnp.linspace(3.0, 6.0, NH_A) + nrm((DEPTH, 2, NH_A), 0.1),
        "g_head_a": gain((DEPTH, NH_A, DH_A)),
        "conv_b_w": nrm((DEPTH, CONV_W, n_xbc), 0.5),
        "conv_b_b": nrm((DEPTH, n_xbc), 0.02),
        "dt_bias": dt0 + jnp.log(-jnp.expm1(-dt0)),
        "a_log": jnp.log(unif((DEPTH, 2, NH_B), 1.0, 16.0)),
        "d_skip": 1.0 + nrm((DEPTH, NH_B), 0.1),
        "g_norm_b": gain((DEPTH, DB)),
        "conv_c_w": nrm((DEPTH, CONV_W, DC), 0.5),
        "conv_c_b": nrm((DEPTH, DC), 0.02),
        "w_rg": nrm((DEPTH, 2, 2, NB_C, BW_C, BW_C), BW_C ** -0.5),
        "b_rg": nrm((DEPTH, 2, 2, DC), 0.1),
        "lam": log_a0 - jnp.log(-jnp.expm1(log_a0)),
        "w_out": nrm((DEPTH, D_MIX, D_MODEL), D_MIX ** -0.5),
        "g_mlp": gain((DEPTH, D_MODEL)),
        "w_mlp1": nrm((DEPTH, D_MODEL, D_FF), D_MODEL ** -0.5),
        "b_mlp1": nrm((DEPTH, D_FF), 0.02),
        "w_mlp2": nrm((DEPTH, D_FF, D_MODEL), D_FF ** -0.5),
        "b_mlp2": nrm((DEPTH, D_MODEL), 0.02),
        "g_final": gain((D_MODEL,)),
    }


def reference(x, c, ctx, c_ctx, w_ada, b_ada, g_mix, w_in, conv_a_w, conv_a_b, b_ig, b_fg, g_head_a,
              conv_b_w, conv_b_b, dt_bias, a_log, d_skip, g_norm_b, conv_c_w, conv_c_b, w_rg, b_rg, lam,
              w_out, g_mlp, w_mlp1, b_mlp1, w_mlp2, b_mlp2, g_final):
    rows = x.shape[1] // GRID_W
    h_ctx = ctx
    for l in range(DEPTH):
        last = l == DEPTH - 1
        sh1, sc1, gt1, sh2, sc2, gt2 = modulation(c, w_ada[l], b_ada[l])
        csh1, csc1, cgt1, csh2, csc2, cgt2 = modulation(c_ctx[None], w_ada[l], b_ada[l])
        p_lat = modulate(rmsnorm(x, g_mix[l]), sh1, sc1) @ w_in[l]
        p_ctx = modulate(rmsnorm(h_ctx, g_mix[l]), csh1, csc1) @ w_in[l]
        y_ctx, y_lat = token_mixers(p_ctx, p_lat, rows, conv_a_w[l], conv_a_b[l], b_ig[l], b_fg[l],
                                    g_head_a[l], conv_b_w[l], conv_b_b[l], dt_bias[l], a_log[l],
                                    d_skip[l], g_norm_b[l], conv_c_w[l], conv_c_b[l], w_rg[l],
                                    b_rg[l], lam[l])
        x = x + gt1 * (y_lat.astype(x.dtype) @ w_out[l])
        x = x + gt2 * sq_relu_mlp(modulate(rmsnorm(x, g_mlp[l]), sh2, sc2),
                                  w_mlp1[l], b_mlp1[l], w_mlp2[l], b_mlp2[l])
        if not last:
            h_ctx = h_ctx + cgt1 * (y_ctx.astype(h_ctx.dtype) @ w_out[l])
            h_ctx = h_ctx + cgt2 * sq_relu_mlp(modulate(rmsnorm(h_ctx, g_mlp[l]), csh2, csc2),
                                              w_mlp1[l], b_mlp1[l], w_mlp2[l], b_mlp2[l])
    return rmsnorm(x, g_final)
```

```python
import math
import numpy as np
import concourse.bass as bass
import concourse.mybir as mybir
from concourse.bass_utils import run_bass_kernel_spmd

F32 = mybir.dt.float32
BF16 = mybir.dt.bfloat16
ALU = mybir.AluOpType
AF = mybir.ActivationFunctionType
AX = mybir.AxisListType

D = 1024
TC = 256
TL = 2048
T = TC + TL
DEPTH = 2
P_IN = 2848
EPS = 1e-6
NPC = 384
LN8 = math.log(0.125)
GELU_C = 2.0 * math.sqrt(2.0 / math.pi)
STRICT_SAME_ENGINE = True

_PC_LAYOUT = [("g_mix", 8), ("g_mlp", 8), ("b_ada", 48), ("b_mlp1", 32), ("b_mlp2", 8),
              ("conv_a_w", 16), ("conv_a_b", 4), ("conv_b_w", 24), ("conv_b_b", 6),
              ("conv_c_w", 8), ("conv_c_b", 2), ("b_rg", 8), ("lam", 4)]
_PC_OFF = {}
_o = 0
for _n, _c in _PC_LAYOUT:
    _PC_OFF[_n] = _o
    _o += _c
PC_PER_LAYER = _o
PC_GFINAL = DEPTH * PC_PER_LAYER

C_ID = 0
C_MF = 128
C_MB = 256
C_GSEL = 384
NCMAIN = 512
C_SEL = 512
C_NSEL = 1536
NCONST = 2560


def _dsize(dt):
    s = str(dt)
    if "bfloat16" in s or "float16" in s:
        return 2
    if "8" in s and "float" in s:
        return 1
    return 4


def _prod(xs):
    r = 1
    for v in xs:
        r *= int(v)
    return r


class _Ins:
    __slots__ = ("stream", "fn", "deps", "dma", "chan", "cval", "inc", "val", "tag")

    def __init__(self, stream, fn, dma=False, chan=None, cval=0):
        self.stream = stream
        self.fn = fn
        self.deps = []
        self.dma = dma
        self.chan = chan
        self.cval = cval
        self.inc = False
        self.val = 0
        self.tag = ""


class Sched:
    STREAMS = ("pe", "act", "dve", "pool", "sp")
    ATTR = {"pe": "tensor", "act": "scalar", "dve": "vector", "pool": "gpsimd", "sp": "sync"}

    def __init__(self, nc):
        self.nc = nc
        self.streams = {s: [] for s in self.STREAMS}
        self.recs = {}
        self.chan_cnt = {}
        self.chan_last = {}
        self.tag = ""
        self.tagmap = None
        self.mloc = {}
        self.n = 0

    def region(self, ap):
        sp = str(ap.space)
        t = ap.tensor
        name = t.name
        steps = ap.ap
        off = ap.offset
        ds = _dsize(ap.dtype)
        if "DRAM" in sp:
            lo = hi = off
            for st, cnt in steps:
                e = st * (cnt - 1)
                if e > 0:
                    hi += e
                else:
                    lo += e
            return (("D", name), 0, 1, lo * ds, (hi + 1) * ds)
        ps = _prod(t.shape[1:])
        p0 = off // ps
        f0 = off - p0 * ps
        pst, pc = steps[0]
        if pc > 1:
            assert pst % ps == 0, (name, steps)
            p1 = p0 + (pc - 1) * (pst // ps) + 1
        else:
            p1 = p0 + 1
        lo = hi = f0
        for st, cnt in steps[1:]:
            e = st * (cnt - 1)
            if e > 0:
                hi += e
            else:
                lo += e
        m = self.mloc.get(name)
        if m is None:
            ml = self.nc.lookup_mloc(name)
            m = (int(ml.addr), int(ml.bank))
            self.mloc[name] = m
        if "PSUM" in sp:
            key = ("P", m[1])
        else:
            key = ("S",)
        return (key, p0, p1, m[0] + lo * ds, m[0] + (hi + 1) * ds)

    def _add_dep(self, ins, d, raw):
        if d is ins:
            return
        if d.stream == ins.stream and not d.dma and not ins.dma:
            if ins.stream == "pe":
                return
            if not raw and not STRICT_SAME_ENGINE:
                return
        ins.deps.append(d)

    def access(self, ins, ap, write):
        key, p0, p1, lo, hi = self.region(ap)
        if key[0] == "P" and ins.stream == "pe" and write:
            p0 = (p0 // 32) * 32
            p1 = ((p1 + 31) // 32) * 32
            lo, hi = 0, 2048
        L = self.recs.get(key)
        if L is None:
            L = []
            self.recs[key] = L
        newL = []
        for r in L:
            rp0, rp1, rlo, rhi, rins, rw = r
            if rp0 < p1 and p0 < rp1 and rlo < hi and lo < rhi:
                if write or rw:
                    self._add_dep(ins, rins, raw=(rw and not write))
                pcov = p0 <= rp0 and rp1 <= p1
                if write and pcov:
                    if rlo < lo:
                        newL.append((rp0, rp1, rlo, lo, rins, rw))
                    if hi < rhi:
                        newL.append((rp0, rp1, hi, rhi, rins, rw))
                    continue
                if (not write) and (not rw) and pcov and lo <= rlo and rhi <= hi \
                        and rins.stream == ins.stream and not rins.dma and not ins.dma:
                    continue
            newL.append(r)
        newL.append((p0, p1, lo, hi, ins, write))
        self.recs[key] = newL

    def op(self, stream, opname, *, extra_r=(), extra_w=(), **kw):
        nm = opname

        def fn(eng, nm=nm, kw=kw):
            return getattr(eng, nm)(**kw)

        ins = _Ins(stream, fn)
        ins.tag = self.tag
        reads, writes = [], []
        for k, v in kw.items():
            if type(v).__name__ != "AP":
                continue
            if k in ("out", "accum_out") or (k == "ap" and nm in ("memset", "memzero")):
                writes.append(v)
            else:
                reads.append(v)
        for v in list(reads) + list(extra_r):
            self.access(ins, v, False)
        for v in list(writes) + list(extra_w):
            self.access(ins, v, True)
        self.streams[stream].append(ins)
        self.n += 1
        return ins

    def dma(self, stream, chan, out, in_, **kw):
        cnt = self.chan_cnt.get(chan, 0) + 1
        self.chan_cnt[chan] = cnt

        def fn(eng, out=out, in_=in_, kw=kw):
            return eng.dma_start(out=out, in_=in_, **kw)

        ins = _Ins(stream, fn, dma=True, chan=chan, cval=16 * cnt)
        ins.tag = self.tag
        prev = self.chan_last.get(chan)
        if prev is not None and not chan.startswith("par"):
            ins.deps.append(prev)
        self.chan_last[chan] = ins
        self.access(ins, in_, False)
        self.access(ins, out, True)
        self.streams[stream].append(ins)
        self.n += 1
        return ins

    def pe(self, opname, **kw):
        return self.op("pe", opname, **kw)

    def act(self, opname, **kw):
        return self.op("act", opname, **kw)

    def dve(self, opname, **kw):
        return self.op("dve", opname, **kw)

    def pool(self, opname, **kw):
        return self.op("pool", opname, **kw)

    def emit(self, stack, final_chans):
        nc = self.nc
        for s in self.STREAMS:
            for ins in self.streams[s]:
                for d in ins.deps:
                    if not d.dma:
                        d.inc = True
        for s in self.STREAMS:
            c = 0
            for ins in self.streams[s]:
                if ins.inc and not ins.dma:
                    c += 1
                    ins.val = c
        esem = {s: stack.enter_context(nc.semaphore("e_" + s)) for s in ("pe", "act", "dve", "pool")}
        csem = {c: stack.enter_context(nc.semaphore("c_" + c)) for c in self.chan_cnt}
        block = stack.enter_context(nc.Block())
        sched = self

        def run(stream, eng):
            known = {}
            for ins in sched.streams[stream]:
                need = {}
                for d in ins.deps:
                    if d.dma:
                        k, v = ("c", d.chan), d.cval
                        if d.chan.startswith("par"):
                            v = 16 * sched.chan_cnt[d.chan]
                    else:
                        k, v = ("e", d.stream), d.val
                    if v > known.get(k, 0) and v > need.get(k, 0):
                        need[k] = v
                for k, v in need.items():
                    sem = csem[k[1]] if k[0] == "c" else esem[k[1]]
                    eng.wait_ge(sem, v)
                    known[k] = v
                bi = ins.fn(eng)
                if sched.tagmap is not None:
                    try:
                        sched.tagmap[str(bi.ins.name)] = ins.tag
                    except Exception:
                        pass
                if ins.dma:
                    bi.then_inc(csem[ins.chan], 16)
                elif ins.inc:
                    bi.then_inc(esem[stream], 1)
            if stream == "sp":
                for c in final_chans:
                    eng.wait_ge(csem[c], 16 * sched.chan_cnt[c])

        @block.tensor
        def _(e):
            run("pe", e)

        @block.scalar
        def _(e):
            run("act", e)

        @block.vector
        def _(e):
            run("dve", e)

        @block.gpsimd
        def _(e):
            run("pool", e)

        @block.sync
        def _(e):
            run("sp", e)


class Arena:
    def __init__(self, nc, stack, nbytes):
        self.nw = nbytes // 4
        self.t = stack.enter_context(nc.sbuf_tensor("arena", [128, self.nw], F32))
        self.top = 0
        self.peak = 0

    def alloc(self, shape, dtype):
        ds = 2 if dtype == BF16 else 4
        nb = _prod(shape[1:]) * ds
        nbr = (nb + 63) // 64 * 64
        lo = self.top
        self.top += nbr
        self.peak = max(self.peak, self.top)
        assert self.top <= self.nw * 4, ("SBUF arena overflow", self.top, self.nw * 4)
        v = self.t[:, lo // 4:(lo + nb + 3) // 4]
        if dtype == BF16:
            v = v.bitcast(BF16)
        n = _prod(shape[1:])
        v = v[:, 0:n]
        if len(shape) == 3:
            v = v.rearrange("p (a b) -> p a b", a=shape[1], b=shape[2])
        elif len(shape) == 4:
            v = v.rearrange("p (a b c) -> p a b c", a=shape[1], b=shape[2], c=shape[3])
        if shape[0] != 128:
            v = v[0:shape[0]]
        return v

    def mark(self):
        return self.top

    def release(self, m):
        self.top = m


def _blocks():
    return [(0, 256)] + [(256 + 512 * i, 512) for i in range(4)]


class _Stop(Exception):
    pass


def build(nlayers=DEPTH, upto=None, taps=(), stop_at=None):
    from contextlib import ExitStack

    def chk(name):
        if stop_at == name:
            raise _Stop()
    nc = bass.Bass("TRN2", target_bir_lowering=False)
    stack = ExitStack()
    S = Sched(nc)

    def din(name, shape, dt=F32):
        return nc.dram_tensor(name, list(shape), dt, kind="ExternalInput").ap()

    xT_in = din("xT", [D, T])
    cvT_in = din("cvT", [128, 8, 2])
    pcols_in = din("pcols", [NPC, 128])
    consts_in = din("consts", [128, NCONST])
    gpar_in = din("gpar", [DEPTH, 2, 8, 4])
    rowp_in = din("rowp", [DEPTH, 1280])
    w_ada = din("w_ada", [DEPTH, D, 6 * D])
    w_in = din("w_in", [DEPTH, D, P_IN])
    w_out = din("w_out", [DEPTH, D, D])
    w_mlp1 = din("w_mlp1", [DEPTH, D, 4 * D])
    w_mlp2 = din("w_mlp2", [DEPTH, 4 * D, D])
    w_rg = din("w_rg", [DEPTH, 2, 2, 4, 64, 64])
    outT = nc.dram_tensor("outT", [D, TL], F32, kind="ExternalOutput").ap()
    xscr = nc.dram_tensor("xscr", [D, T], F32, kind="Internal").ap()
    tap_out = {}
    final_chans = []

    A = Arena(nc, stack, 211968)
    PS = [stack.enter_context(nc.psum_tensor("psb%d" % i, [128, 512], F32)) for i in range(8)]

    def ps(i, lo=0, hi=512, p=128):
        return PS[i][0:p, lo:hi]

    def ps_bf(i):
        return PS[i][:, :].bitcast(BF16)

    def tap(name, ap, stream="sp"):
        if name not in taps:
            return
        shp = list(ap.shape)
        dt_ = ap.dtype
        t = nc.dram_tensor("tap_" + name, shp, dt_, kind="ExternalOutput").ap()
        tap_out[name] = (shp, dt_)
        ch = "tap_" + name
        S.dma(stream, ch, out=t, in_=ap)
        final_chans.append(ch)

    CONST = A.alloc([128, NCMAIN], F32)
    PC = A.alloc([128, NPC], F32)
    XN = A.alloc([128, 8, T], BF16)
    Y = A.alloc([128, 8, T], BF16)
    IDB = A.alloc([128, 128], BF16)
    ONESB = A.alloc([128, 128], BF16)
    MADD = A.alloc([128, 2, 128], BF16)
    CSS = A.alloc([128, 8, 2], F32)
    CSSB = A.alloc([128, 8, 2], BF16)
    MOD = A.alloc([128, 48, 2], F32)
    A1 = A.alloc([128, 8, 2], F32)
    A2 = A.alloc([128, 8, 2], F32)
    GB2 = A.alloc([128, 8, 2], F32)
    ROWP = A.alloc([128, 1280], F32)
    GP = A.alloc([8, 2, 4], F32)
    GPX = A.alloc([8, 2, 4], F32)
    CL = A.alloc([128, 4, 2], F32)
    SMASK = A.alloc([8, 2, 512], F32)
    IDENT = CONST[:, C_ID:C_ID + 128]
    SELB = A.alloc([128, 2048], BF16)

    S.dma("sp", "par", out=CONST, in_=consts_in[:, 0:NCMAIN])
    _m0 = A.mark()
    PCT = A.alloc([128, 3, 128], F32)
    SELF = A.alloc([128, 2048], F32)
    A.release(_m0)
    S.dma("sp", "par", out=SELF, in_=consts_in[:, C_SEL:C_SEL + 2048])
    S.dma("sp", "par", out=PCT, in_=pcols_in.rearrange("(a p) c -> p a c", p=128))
    S.dma("sp", "par", out=CSS, in_=cvT_in)
    for a in range(3):
        S.pe("transpose", out=ps(0, a * 128, a * 128 + 128), in_=PCT[:, a, :], identity=IDENT)
    S.dve("tensor_copy", out=PC, in_=ps(0, 0, 384))
    S.dve("tensor_copy", out=IDB, in_=IDENT)
    S.dve("memset", ap=ONESB, constant=1.0)
    S.dve("tensor_copy", out=SELB, in_=SELF)
    S.dve("tensor_copy", out=MADD, in_=CONST[:, C_MF:C_MF + 256].rearrange("p (a b) -> p a b", a=2))
    S.act("activation", out=CSS, in_=CSS, func=AF.Silu)
    S.dve("tensor_copy", out=CSSB, in_=CSS)
    S.dve("memset", ap=SMASK, constant=1.0)
    S.dve("memset", ap=SMASK[:, 0, :].rearrange("p (c l) -> p c l", l=128)[:, :, 0:1], constant=0.0)
    S.dve("memset", ap=SMASK[:, 1, :].rearrange("p (c l) -> p c l", l=128)[:, :, 127:128], constant=0.0)

    def pc(l, name, idx=0, n=1):
        o = l * PC_PER_LAYER + _PC_OFF[name] + idx
        return PC[:, o:o + n]

    base_mark = A.mark()
    if "zy" in taps:
        S.pool("memset", ap=Y, constant=0.0)

    def phase_mod(l):
        WA = [A.alloc([128, 8, 512], BF16) for _ in range(3)]
        MODR = A.alloc([2, 6 * D], F32)
        for jg in range(12):
            slot = WA[jg % 3]
            S.dma("pool", "wa%d" % (jg % 3), out=slot,
                  in_=w_ada[l][:, jg * 512:(jg + 1) * 512].rearrange("(k p) n -> p k n", p=128))
            bank = 2 + jg % 4
            for k in range(8):
                S.pe("matmul", out=ps(bank, 0, 512, p=2), lhsT=CSSB[:, k, :], rhs=slot[:, k, :], start=(k == 0), stop=(k == 7))
            S.act("activation", out=MODR[:, jg * 512:(jg + 1) * 512], in_=ps(bank, 0, 512, p=2), func=AF.Copy)
        for j in range(48):
            S.pe("matmul", out=ps(1, 2 * j, 2 * j + 2), lhsT=MODR[0:2, j * 128:(j + 1) * 128], rhs=IDENT[0:2, 0:2],
                 start=True, stop=True)
        S.dve("tensor_tensor", out=MOD, in0=ps(1, 0, 96).rearrange("p (a b) -> p a b", b=2),
              in1=pc(l, "b_ada", 0, 48).unsqueeze(2).to_broadcast([128, 48, 2]), op=ALU.add)
        S.dve("tensor_scalar", out=A1, in0=MOD[:, 8:16, :], scalar1=1.0, scalar2=None, op0=ALU.add)
        S.dve("tensor_tensor", out=A1, in0=A1, in1=pc(l, "g_mix", 0, 8).unsqueeze(2).to_broadcast([128, 8, 2]),
              op=ALU.mult)
        S.dve("tensor_scalar", out=A2, in0=MOD[:, 32:40, :], scalar1=1.0, scalar2=None, op0=ALU.add)
        S.dve("tensor_tensor", out=A2, in0=A2, in1=pc(l, "g_mlp", 0, 8).unsqueeze(2).to_broadcast([128, 8, 2]),
              op=ALU.mult)
        S.dve("tensor_tensor", out=GB2, in0=MOD[:, 40:48, :],
              in1=pc(l, "b_mlp2", 0, 8).unsqueeze(2).to_broadcast([128, 8, 2]), op=ALU.mult)
        S.dma("sp", "par_m%d" % l, out=ROWP, in_=rowp_in[l:l + 1, :].partition_broadcast(128))
        S.dma("sp", "par_m%d" % l, out=GP, in_=gpar_in[l].rearrange("d r k -> r d k"))
        S.dve("tensor_scalar", out=GPX[:, :, 0:1], in0=GP[:, :, 0:1], scalar1=LN8, scalar2=None, op0=ALU.add)
        S.dve("tensor_scalar", out=GPX[:, :, 1:2], in0=GP[:, :, 1:2], scalar1=-1.0, scalar2=None, op0=ALU.mult)
        S.dve("tensor_copy", out=GPX[:, :, 2:3], in_=GP[:, :, 2:3])
        S.act("activation", out=GPX[:, :, 3:4], in_=GP[:, :, 3:4], func=AF.Exp)
        S.dve("tensor_scalar", out=GPX[:, :, 3:4], in0=GPX[:, :, 3:4], scalar1=-1.0, scalar2=None, op0=ALU.mult)
        S.act("activation", out=CL[:, :, 0], in_=pc(l, "lam", 0, 4), func=AF.Exp, scale=-1.0)
        S.act("activation", out=CL[:, :, 0], in_=CL[:, :, 0], func=AF.Ln, bias=1.0)
        S.dve("tensor_scalar", out=CL[:, :, 1], in0=CL[:, :, 0], scalar1=-16.0, scalar2=None, op0=ALU.mult)
        S.dve("tensor_scalar", out=CL[:, :, 0], in0=CL[:, :, 0], scalar1=-8.0, scalar2=None, op0=ALU.mult)

    def xsrc(l):
        return xT_in if l == 0 else xscr

    def phase_norm_stats(l, src):
        XS = [A.alloc([128, 8, 512], F32) for _ in range(2)]
        SQ = A.alloc([128, 8, 512], BF16)
        RSA = A.alloc([128, T], F32)
        for bi, (t0, n) in enumerate(_blocks()):
            xs = XS[bi % 2]
            S.dma("sp", "xs%d" % (bi % 2), out=xs[:, :, 0:n], in_=src[:, t0:t0 + n].rearrange("(k p) t -> p k t", p=128))
            S.act("activation", out=SQ[:, :, 0:n], in_=xs[:, :, 0:n], func=AF.Square)
            for k in range(8):
                S.pe("matmul", out=ps(0, 0, n), lhsT=ONESB, rhs=SQ[:, k, 0:n], start=(k == 0), stop=(k == 7))
            S.act("activation", out=RSA[:, t0:t0 + n], in_=ps(0, 0, n), func=AF.Ln, scale=1.0 / D, bias=EPS)
            S.act("activation", out=RSA[:, t0:t0 + n], in_=RSA[:, t0:t0 + n], func=AF.Exp, scale=-0.5)
        return XS, RSA

    def phase_norm_apply(l, src, which, amod, XS, RSA):
        TMP = A.alloc([128, 8, 512], F32)
        for bi, (t0, n) in enumerate(_blocks()):
            j = 1 if t0 < TC else 0
            xs = XS[bi % 2]
            S.dma("sp", "xs%d" % (bi % 2), out=xs[:, :, 0:n], in_=src[:, t0:t0 + n].rearrange("(k p) t -> p k t", p=128))
            S.dve("tensor_tensor", out=TMP[:, :, 0:n], in0=xs[:, :, 0:n],
                  in1=RSA[:, t0:t0 + n].unsqueeze(1).to_broadcast([128, 8, n]), op=ALU.mult)
            for k in range(8):
                S.act("activation", out=XN[:, k, t0:t0 + n], in_=TMP[:, k, 0:n], func=AF.Identity,
                      scale=amod[:, k, j:j + 1], bias=MOD[:, which * 8 + k, j:j + 1])

    def load_w(slot, chan, src2d, c0, ncols):
        S.dma("pool", chan, out=slot[:, :, 0:ncols],
              in_=src2d[:, c0:c0 + ncols].rearrange("(k p) n -> p k n", p=128))

    def proj_fm(psum_ap, wslot, M, t0, n):
        for k in range(8):
            S.pe("matmul", out=psum_ap, lhsT=wslot[:, k, 0:M], rhs=XN[:, k, t0:t0 + n],
                 start=(k == 0), stop=(k == 7))

    def proj_tm(psum_ap, wslot, c0, ncols, t0):
        for k in range(8):
            S.pe("matmul", out=psum_ap, lhsT=XN[:, k, t0:t0 + 128], rhs=wslot[:, k, c0:c0 + ncols],
                 start=(k == 0), stop=(k == 7))

    def conv_tile(PRE, OUT, wcol, bcol):
        for (o0, n, p0) in ((0, TC, 0), (TC, TL, 259)):
            S.dve("tensor_scalar", out=OUT[:, o0:o0 + n], in0=PRE[:, p0:p0 + n], scalar1=wcol(0), scalar2=bcol,
                  op0=ALU.mult, op1=ALU.add)
            for j in range(1, 4):
                S.dve("scalar_tensor_tensor", out=OUT[:, o0:o0 + n], in0=PRE[:, p0 + j:p0 + j + n], scalar=wcol(j),
                      in1=OUT[:, o0:o0 + n], op0=ALU.mult, op1=ALU.add)

    def zero_halo(PRE):
        S.pool("memset", ap=PRE[:, 0:2], constant=0.0)
        S.pool("memset", ap=PRE[:, 258:261], constant=0.0)
        S.pool("memset", ap=PRE[:, 2309:2312], constant=0.0)

    def mixer_c(l):
        m = A.mark()
        WBD = A.alloc([128, 8, 128], F32)
        S.pool("memset", ap=WBD, constant=0.0)
        for d in range(2):
            for g in range(2):
                for nb in range(4):
                    j, hb = nb // 2, nb % 2
                    S.dma("sp", "par_w%d" % l, out=WBD[64 * hb:64 * hb + 64, d * 4 + g * 2 + j, 64 * hb:64 * hb + 64],
                          in_=w_rg[l, d, g, nb])
        WS = [A.alloc([128, 8, 128], BF16) for _ in range(2)]
        PRE = A.alloc([128, 2312], F32)
        GR = A.alloc([128, T], F32)
        XC = A.alloc([128, T], F32)
        HS = A.alloc([128, T], F32)
        AA = A.alloc([128, T], F32)
        UU = A.alloc([128, T], F32)
        RGf = A.alloc([128, T], F32)
        IGf = A.alloc([128, T], F32)
        H1 = PRE[:, 0:T]
        for j in range(2):
            load_w(WS[0], "wc0", w_in[l], 2336 + 128 * j, 128)
            load_w(WS[1], "wc1", w_in[l], 2592 + 128 * j, 128)
            zero_halo(PRE)
            for (t0, n) in _blocks():
                proj_fm(ps(2, 0, n), WS[0], 128, t0, n)
                if t0 < TC:
                    S.act("activation", out=PRE[:, 2:2 + TC], in_=ps(2, 0, n), func=AF.Copy)
                else:
                    r0 = (t0 - TC) // 64
                    S.act("activation",
                          out=PRE[:, 261:261 + TL].rearrange("p (w r) -> p w r", r=32)[:, :, r0:r0 + 8],
                          in_=ps(2, 0, 512).rearrange("p (r w) -> p w r", w=64), func=AF.Copy)
                proj_fm(ps(3, 0, n), WS[1], 128, t0, n)
                S.dve("tensor_copy", out=GR[:, t0:t0 + n], in_=ps(3, 0, n))
            conv_tile(PRE, XC, lambda q: pc(l, "conv_c_w", q * 2 + j), pc(l, "conv_c_b", j))
            for d in range(2):
                for bi_, (t0, n) in enumerate(_blocks()):
                    b4, b5 = (4, 5) if bi_ % 2 == 0 else (6, 7)
                    S.pe("matmul", out=ps(b4, 0, n), lhsT=WBD[:, d * 4 + j, :], rhs=XC[:, t0:t0 + n], start=True, stop=True)
                    S.pe("matmul", out=ps(b5, 0, n), lhsT=WBD[:, d * 4 + 2 + j, :], rhs=XC[:, t0:t0 + n], start=True, stop=True)
                    S.act("activation", out=RGf[:, t0:t0 + n], in_=ps(b4, 0, n), func=AF.Sigmoid, bias=pc(l, "b_rg", d * 4 + j))
                    S.act("activation", out=IGf[:, t0:t0 + n], in_=ps(b5, 0, n), func=AF.Sigmoid, bias=pc(l, "b_rg", d * 4 + 2 + j))
                S.act("activation", out=AA, in_=RGf, func=AF.Exp, scale=CL[:, d * 2 + j, 0:1])
                S.act("activation", out=RGf, in_=RGf, func=AF.Exp, scale=CL[:, d * 2 + j, 1:2])
                S.act("activation", out=RGf, in_=RGf, func=AF.Sqrt, scale=-1.0, bias=1.0)
                S.dve("tensor_tensor", out=IGf, in0=IGf, in1=RGf, op=ALU.mult)
                S.dve("tensor_tensor", out=UU, in0=IGf, in1=XC, op=ALU.mult)
                if d == 0:
                    S.dve("tensor_tensor_scan", out=HS, data0=AA, data1=UU, initial=0.0, op0=ALU.mult, op1=ALU.add)
                else:
                    S.dve("tensor_tensor_scan", out=H1[:, 0:TC][:, ::-1], data0=AA[:, 0:TC][:, ::-1],
                          data1=UU[:, 0:TC][:, ::-1], initial=0.0, op0=ALU.mult, op1=ALU.add)
                    S.dve("tensor_tensor_scan", out=H1[:, TC:T][:, ::-1], data0=AA[:, TC:T][:, ::-1],
                          data1=UU[:, TC:T][:, ::-1], initial=H1[:, 0:1], op0=ALU.mult, op1=ALU.add)
                    S.pool("tensor_tensor", out=HS, in0=HS, in1=H1, op=ALU.add)
            S.act("activation", out=AA, in_=GR, func=AF.Square)
            S.dve("tensor_scalar", out=AA, in0=AA, scalar1=0.044715, scalar2=1.0, op0=ALU.mult, op1=ALU.add)
            S.pool("tensor_tensor", out=AA, in0=AA, in1=GR, op=ALU.mult)
            S.act("activation", out=AA, in_=AA, func=AF.Sigmoid, scale=GELU_C)
            S.pool("tensor_tensor", out=GR, in0=AA, in1=GR, op=ALU.mult)
            S.dve("tensor_tensor", out=Y[:, 6 + j, 0:TC], in0=HS[:, 0:TC], in1=GR[:, 0:TC], op=ALU.mult)
            S.dve("tensor_tensor", out=Y[:, 6 + j, TC:T].rearrange("p (r w) -> p r w", w=64),
                  in0=HS[:, TC:T].rearrange("p (w r) -> p r w", r=32),
                  in1=GR[:, TC:T].rearrange("p (r w) -> p r w", w=64), op=ALU.mult)
        A.release(m)

    def gate_post(G, EL, R, n, d, GW, par=0):
        nch = n // 128
        EG, TW, DECX, TG, DECB = GW
        e = 127 if d == 0 else 0
        G3 = G[0:R, 0:n].rearrange("p (c l) -> p c l", l=128)
        EG3 = EG[0:R, 0:n].rearrange("p (c l) -> p c l", l=128)
        S.act("activation", out=EG[0:R, 0:n], in_=G[0:R, 0:n], func=AF.Exp)
        S.dve("tensor_tensor", out=TW[0:R, 0:n].rearrange("p (c l) -> p c l", l=128),
              in0=G3[:, :, e:e + 1].to_broadcast([R, nch, 128]), in1=G3, op=ALU.subtract)
        S.act("activation", out=TW[0:R, 0:n], in_=TW[0:R, 0:n], func=AF.Exp)
        S.dve("tensor_tensor", out=DECX[0:R, 0:R, 0:nch],
              in0=EG3[:, :, e].unsqueeze(1).to_broadcast([R, R, nch]),
              in1=IDENT[0:R, 0:R].unsqueeze(2).to_broadcast([R, R, nch]), op=ALU.mult)
        return TG[par], DECB[par]

    def gate_post2(EL, R, n, GW, par=0):
        nch = n // 128
        EG, TW, DECX, TG, DECB = GW
        TG = TG[par]
        DECB = DECB[par]
        S.pe("matmul", out=ps(0, 0, R * nch).rearrange("p (u c) -> p u c", c=nch), lhsT=CONST[0:R, C_GSEL:C_GSEL + 128],
             rhs=DECX[0:R, 0:R, 0:nch], start=True, stop=True)
        S.act("activation", out=DECB[:, 0:R, 0:nch], in_=ps(0, 0, R * nch).rearrange("p (u c) -> p u c", c=nch),
              func=AF.Copy)
        for ci in range(nch):
            for q, SRC in enumerate((EL, TW, EG)):
                o = (ci * 3 + q) * R
                S.pe("matmul", out=ps(1, o, o + R), lhsT=SRC[0:R, ci * 128:(ci + 1) * 128],
                     rhs=IDENT[0:R, 0:R], start=True, stop=True)
        S.dve("tensor_copy", out=TG[:, 0:nch, :, 0:R],
              in_=ps(1, 0, nch * 3 * R).rearrange("p (c q r) -> p c q r", q=3, r=R))

    def alloc_gate_ws():
        EL = A.alloc([8, 512], F32)
        T1 = A.alloc([8, 512], F32)
        G = A.alloc([8, 512], F32)
        EG = A.alloc([8, 512], F32)
        TW = A.alloc([8, 512], F32)
        DECX = A.alloc([8, 8, 4], F32)
        TG = [A.alloc([128, 4, 3, 8], F32) for _ in range(2)]
        DECB = [A.alloc([128, 8, 4], F32) for _ in range(2)]
        return EL, T1, G, (EG, TW, DECX, TG, DECB)

    def sweep_blocks(d):
        blks = _blocks()
        order = [0, 1, 2, 3, 4] if d == 0 else [0, 4, 3, 2, 1]
        for bi in order:
            t0, n = blks[bi]
            nch = n // 128
            cis = list(range(nch)) if d == 0 else list(range(nch - 1, -1, -1))
            yield t0, n, cis

    def gdiff(bank, j, u, GH, GL, c0, d):
        o = ps(bank, j * 128, (j + 1) * 128)
        sel = SELB[:, u * 128:(u + 1) * 128]
        nsel = SELB[:, 1024 + u * 128:1024 + (u + 1) * 128]
        S.pe("matmul", out=o, lhsT=sel, rhs=GH[:, c0:c0 + 128], start=True, stop=False)
        S.pe("matmul", out=o, lhsT=sel, rhs=GL[:, c0:c0 + 128], start=False, stop=False)
        S.pe("matmul", out=o, lhsT=GH[:, c0:c0 + 128], rhs=nsel, start=False, stop=False)
        S.pe("matmul", out=o, lhsT=GL[:, c0:c0 + 128], rhs=nsel, start=False, stop=False)
        S.pe("matmul", out=o, lhsT=IDB, rhs=MADD[:, d, :], start=False, stop=True)

    def split_hilo(G, GH, GL, R, n):
        S.act("activation", out=GH[0:R, 0:n], in_=G[0:R, 0:n], func=AF.Copy)
        S.dve("tensor_tensor", out=GL[0:R, 0:n], in0=G[0:R, 0:n], in1=GH[0:R, 0:n], op=ALU.subtract)

    def mixer_a(l):
        m = A.mark()
        QZ = A.alloc([128, 4, T], BF16)
        KT = A.alloc([128, 2, T], BF16)
        S.pool("memset", ap=QZ, constant=0.0)
        WVO = A.alloc([128, 8, 512], BF16)
        GWA = A.alloc([128, 8, 16], BF16)
        YB = A.alloc([128, 18, 256], BF16)
        load_w(WVO, "wvo", w_in[l], 512, 512)
        load_w(GWA, "gwa", w_in[l], 1024, 16)
        m2 = A.mark()
        PRE = A.alloc([128, 2312], F32)
        CT = A.alloc([128, T], F32)
        WS = [A.alloc([128, 8, 128], BF16) for _ in range(2)]
        zero_halo(PRE)
        for ti in range(4):
            load_w(WS[ti % 2], "wc%d" % (ti % 2), w_in[l], 128 * ti, 128)
            for (t0, n) in _blocks():
                proj_fm(ps(2 + (t0 // 512) % 2, 0, n), WS[ti % 2], 128, t0, n)
                dst = PRE[:, 2 + t0:2 + t0 + n] if t0 < TC else PRE[:, 261 + t0 - TC:261 + t0 - TC + n]
                S.act("activation", out=dst, in_=ps(2 + (t0 // 512) % 2, 0, n), func=AF.Copy)
            conv_tile(PRE, CT, lambda q: pc(l, "conv_a_w", q * 4 + ti), pc(l, "conv_a_b", ti))
            if ti < 2:
                S.act("activation", out=QZ[0:64, 2 * ti, :], in_=CT[0:64, :], func=AF.Silu)
                S.act("activation", out=QZ[64:128, 2 * ti + 1, :], in_=CT[64:128, :], func=AF.Silu)
            else:
                S.act("activation", out=KT[:, ti - 2, :], in_=CT, func=AF.Silu)
        A.release(m2)
        EL, T1, G, GW = alloc_gate_ws()
        GHs = [A.alloc([128, 512], BF16) for _ in range(2)]
        GLs = [A.alloc([128, 512], BF16) for _ in range(2)]
        for _g in GHs + GLs:
            S.pool("memset", ap=_g, constant=0.0)
        CS = A.alloc([128, 4, 65], F32)
        CB = A.alloc([128, 4, 65], BF16)
        BLK = A.alloc([128, 4, 65], F32)
        S.pool("memset", ap=BLK, constant=0.0)
        for h in range(4):
            S.pool("memset", ap=BLK[64 * (h % 2):64 * (h % 2) + 64, h, :], constant=1.0)
        VA = A.alloc([128, 4, 65], BF16)
        S.pool("memset", ap=VA, constant=1.0)
        OG = [A.alloc([128, 256], F32) for _ in range(3)]
        IC = [A.alloc([128, 4, 65], F32) for _ in range(3)]
        UC = [A.alloc([128, 4, 65], F32) for _ in range(2)]
        KTMs = [A.alloc([128, 256], BF16) for _ in range(2)]
        VHs = [A.alloc([128, 4, 65], BF16) for _ in range(2)]
        VTs = [A.alloc([128, 4, 65], BF16) for _ in range(2)]
        DTts = [A.alloc([128, 512], BF16) for _ in range(2)]
        STms = [A.alloc([128, 4, 128], BF16) for _ in range(2)]
        SSBs = [A.alloc([128, 4, 128], BF16) for _ in range(2)]
        YTs = [A.alloc([128, 4, 65], F32) for _ in range(2)]
        DN = A.alloc([128, 4, 1], F32)
        HV = A.alloc([128, 4, 64], F32)
        HQ = A.alloc([128, 4, 64], F32)
        SSq = A.alloc([128, 4], F32)
        YA = [A.alloc([128, 256], BF16) for _ in range(2)]

        def gate_prep(t0, n, d, par):
            proj_fm(ps(0, 0, n, p=4), GWA[:, :, 4 * d:4 * d + 4], 4, t0, n)
            proj_fm(ps(1, 0, n, p=4), GWA[:, :, 8 + 4 * d:8 + 4 * d + 4], 4, t0, n)
            S.act("activation", out=EL[0:4, 0:n], in_=ps(0, 0, n, p=4), func=AF.Exp, bias=GPX[0:4, d, 0:1])
            S.act("activation", out=T1[0:4, 0:n], in_=ps(1, 0, n, p=4), func=AF.Exp, scale=-1.0, bias=GPX[0:4, d, 1:2])
            S.act("activation", out=T1[0:4, 0:n], in_=T1[0:4, 0:n], func=AF.Ln, bias=1.0)
            if d == 0:
                S.dve("tensor_tensor_scan", out=G[0:4, 0:n], data0=SMASK[0:4, 0, 0:n], data1=T1[0:4, 0:n],
                      initial=0.0, op0=ALU.mult, op1=ALU.subtract)
            else:
                S.dve("tensor_tensor_scan", out=G[0:4, 0:n][:, ::-1], data0=SMASK[0:4, 1, 0:n][:, ::-1],
                      data1=T1[0:4, 0:n][:, ::-1], initial=0.0, op0=ALU.mult, op1=ALU.subtract)
            split_hilo(G, GHs[par], GLs[par], 4, n)
            TG, DECB = gate_post(G, EL, 4, n, d, GW, par)
            return [TG, DECB, GHs[par], GLs[par], (lambda: gate_post2(EL, 4, n, GW, par))]

        def front(ch, d, final):
            tc, ci, TG, DECB, par, GH, GL, par3 = ch
            KTM, VH, VT, DTt, STm = KTMs[par], VHs[par], VTs[par], DTts[par], STms[par]
            ncol = 512 if final else 256
            proj_tm(ps(2, 0, ncol), WVO, 0, ncol, tc)
            S.act("activation", out=VA[:, :, 0:64], in_=ps(2, 0, 256).rearrange("p (h e) -> p h e", e=64), func=AF.Copy)
            if final:
                S.act("activation", out=OG[par3], in_=ps(2, 256, 512), func=AF.Exp, scale=-1.0)
                S.act("activation", out=OG[par3], in_=OG[par3], func=AF.Ln, bias=1.0)
                S.act("activation", out=OG[par3], in_=OG[par3], func=AF.Exp, scale=-1.0)
            for u in range(4):
                gdiff(5, u, u, GH, GL, ci * 128, d)
            S.act("activation", out=DTt, in_=ps(5), func=AF.Exp)
            for h in range(4):
                S.pe("matmul", out=ps(4, h * 128, (h + 1) * 128), lhsT=KT[:, h // 2, tc:tc + 128],
                     rhs=QZ[:, h, tc:tc + 128], start=True, stop=True)
            S.dve("tensor_tensor", out=STm, in0=DTt.rearrange("p (u t) -> p u t", t=128),
                  in1=ps(4).rearrange("p (u t) -> p u t", t=128), op=ALU.mult)
            for tt in range(2):
                S.pe("transpose", out=ps_bf(3)[:, tt * 128:(tt + 1) * 128], in_=KT[:, tt, tc:tc + 128], identity=IDB)
            S.dve("tensor_copy", out=KTM, in_=ps_bf(3)[:, 0:256])
            S.pool("tensor_tensor", out=VH, in0=VA, in1=TG[:, ci, 0, 0:4].unsqueeze(2).to_broadcast([128, 4, 65]), op=ALU.mult)
            S.pool("tensor_tensor", out=VT, in0=VH, in1=TG[:, ci, 1, 0:4].unsqueeze(2).to_broadcast([128, 4, 65]), op=ALU.mult)
            for tt in range(2):
                S.pe("matmul", out=ps(0, tt * 130, tt * 130 + 130), lhsT=KTM[:, tt * 128:(tt + 1) * 128],
                     rhs=VT[:, 2 * tt:2 * tt + 2, :], start=True, stop=True)
            S.dve("tensor_tensor", out=UC[par], in0=ps(0, 0, 260).rearrange("p (u v) -> p u v", v=65), in1=BLK, op=ALU.mult)
            for u in range(4):
                S.pe("matmul", out=ps(6, u * 65, u * 65 + 65), lhsT=STm[:, u, :], rhs=VH[:, u, :], start=True, stop=True)
            S.act("activation", out=IC[par3], in_=ps(6, 0, 260).rearrange("p (u v) -> p u v", v=65), func=AF.Copy)

        def back(ch, d, final):
            tc, ci, TG, DECB, par, GH, GL, par3 = ch
            YT = YTs[par]
            for h in range(4):
                S.pe("matmul", out=ps(7, h * 65, h * 65 + 65), lhsT=QZ[:, h, tc:tc + 128], rhs=CB[:, h, :], start=True, stop=True)
            S.dve("tensor_tensor", out=YT, in0=ps(7, 0, 260).rearrange("p (u v) -> p u v", v=65),
                  in1=TG[:, ci, 2, 0:4].unsqueeze(2).to_broadcast([128, 4, 65]), op=ALU.mult)
            S.pool("tensor_tensor", out=CS, in0=CS, in1=DECB[:, 0:4, ci:ci + 1].to_broadcast([128, 4, 65]), op=ALU.mult)
            S.pool("tensor_tensor", out=CS, in0=CS, in1=UC[par], op=ALU.add)
            S.act("activation", out=CB, in_=CS, func=AF.Copy)

        def epi(ch, d, final):
            tc, ci, TG, DECB, par, GH, GL, par3 = ch
            c = tc // 128
            YT = YTs[par]
            S.dve("tensor_tensor", out=YT, in0=YT, in1=IC[par3], op=ALU.add)
            S.dve("tensor_scalar", out=DN, in0=YT[:, :, 64:65], scalar1=-1.0, scalar2=1.0, op0=ALU.mult, op1=ALU.max)
            S.dve("tensor_tensor", out=DN, in0=DN, in1=YT[:, :, 64:65], op=ALU.max)
            S.dve("reciprocal", out=DN, in_=DN)
            if not final:
                S.dve("tensor_tensor", out=YB[:, c, :].rearrange("p (u v) -> p u v", v=64), in0=YT[:, :, 0:64],
                      in1=DN.to_broadcast([128, 4, 64]), op=ALU.mult)
            else:
                S.dve("tensor_tensor", out=HV, in0=YT[:, :, 0:64], in1=DN.to_broadcast([128, 4, 64]), op=ALU.mult)
                S.dve("tensor_tensor", out=HV, in0=HV, in1=YB[:, c, :].rearrange("p (u v) -> p u v", v=64), op=ALU.add)
                S.dve("tensor_tensor", out=HQ, in0=HV, in1=HV, op=ALU.mult)
                S.dve("tensor_reduce", out=SSq, in_=HQ, axis=AX.X, op=ALU.add)
                S.act("activation", out=SSq, in_=SSq, func=AF.Ln, scale=1.0 / 64, bias=EPS)
                S.act("activation", out=SSq, in_=SSq, func=AF.Exp, scale=-0.5)
                S.dve("tensor_tensor", out=HV, in0=HV, in1=SSq.unsqueeze(2).to_broadcast([128, 4, 64]), op=ALU.mult)
                S.dve("tensor_tensor", out=HV, in0=HV, in1=ROWP[:, 0:256].rearrange("p (u v) -> p u v", v=64), op=ALU.mult)
                S.dve("tensor_tensor", out=YA[par].rearrange("p (u v) -> p u v", v=64), in0=HV,
                      in1=OG[par3].rearrange("p (u v) -> p u v", v=64), op=ALU.mult)

        def back2(ch, d, final):
            tc, ci, TG, DECB, par, GH, GL, par3 = ch
            if final:
                for tt in range(2):
                    S.pe("transpose", out=ps_bf(1)[:, 512 + tt * 128:512 + (tt + 1) * 128],
                         in_=YA[par][:, tt * 128:(tt + 1) * 128], identity=IDB)
                S.act("activation", out=Y[:, 0:2, tc:tc + 128],
                      in_=ps_bf(1)[:, 512:768].rearrange("p (a b) -> p a b", a=2), func=AF.Copy)

        for d in (1, 0):
            final = d == 0
            S.pool("memset", ap=CS, constant=0.0)
            S.pool("memset", ap=CB, constant=0.0)
            blist = list(sweep_blocks(d))
            gates = {0: gate_prep(blist[0][0], blist[0][1], d, 0)}
            gates[0][4]()
            q = [None, None, None]
            cnt = 0

            def advance(newch):
                if q[0] is not None:
                    back(q[0], d, final)
                if q[1] is not None:
                    epi(q[1], d, final)
                if q[2] is not None:
                    back2(q[2], d, final)
                q[2] = q[1]
                q[1] = q[0]
                q[0] = newch

            for bidx, (t0, n, cis) in enumerate(blist):
                TG, DECB, GH, GL, _p2 = gates.pop(bidx)
                for k_, ci in enumerate(cis):
                    ch = (t0 + ci * 128, ci, TG, DECB, cnt % 2, GH, GL, cnt % 3)
                    cnt += 1
                    front(ch, d, final)
                    if q[0] is not None:
                        back(q[0], d, final)
                    if k_ == 0 and bidx + 1 < len(blist):
                        gates[bidx + 1] = gate_prep(blist[bidx + 1][0], blist[bidx + 1][1], d, (bidx + 1) % 2)
                    if k_ == 1 and bidx + 1 < len(blist):
                        gates[bidx + 1][4]()
                    if q[1] is not None:
                        epi(q[1], d, final)
                    if q[2] is not None:
                        back2(q[2], d, final)
                    q[2] = q[1]
                    q[1] = q[0]
                    q[0] = ch
            for _ in range(3):
                advance(None)
        A.release(m)

    def mixer_b(l):
        m = A.mark()
        XT4 = A.alloc([128, 4, T], BF16)
        BT = A.alloc([128, T], BF16)
        CZ = A.alloc([128, 2, T], BF16)
        WZ = A.alloc([128, 8, 512], BF16)
        GWB = A.alloc([128, 8, 16], BF16)
        YB = A.alloc([128, 18, 512], BF16)
        S.pool("memset", ap=CZ, constant=0.0)
        load_w(WZ, "wz", w_in[l], 1040, 512)
        load_w(GWB, "gwb", w_in[l], 2320, 16)
        m2 = A.mark()
        PRE = A.alloc([128, 2312], F32)
        CT = A.alloc([128, T], F32)
        WS = [A.alloc([128, 8, 128], BF16) for _ in range(2)]
        zero_halo(PRE)
        for ti in range(6):
            c0 = 1552 + 128 * ti if ti < 4 else (2064 if ti == 4 else 2192)
            load_w(WS[ti % 2], "wc%d" % (ti % 2), w_in[l], c0, 128)
            for (t0, n) in _blocks():
                proj_fm(ps(2 + (t0 // 512) % 2, 0, n), WS[ti % 2], 128, t0, n)
                dst = PRE[:, 2 + t0:2 + t0 + n] if t0 < TC else PRE[:, 261 + t0 - TC:261 + t0 - TC + n]
                S.act("activation", out=dst, in_=ps(2 + (t0 // 512) % 2, 0, n), func=AF.Copy)
            conv_tile(PRE, CT, lambda q: pc(l, "conv_b_w", q * 6 + ti), pc(l, "conv_b_b", ti))
            if ti < 4:
                S.act("activation", out=XT4[:, ti, :], in_=CT, func=AF.Silu)
            elif ti == 4:
                S.act("activation", out=BT, in_=CT, func=AF.Silu)
            else:
                S.act("activation", out=CZ[0:64, 0, :], in_=CT[0:64, :], func=AF.Silu)
                S.act("activation", out=CZ[64:128, 1, :], in_=CT[64:128, :], func=AF.Silu)
        A.release(m2)
        EL, T1, G, GW = alloc_gate_ws()
        GHs = [A.alloc([128, 512], BF16) for _ in range(2)]
        GLs = [A.alloc([128, 512], BF16) for _ in range(2)]
        for _g in GHs + GLs:
            S.pool("memset", ap=_g, constant=0.0)
        CS = A.alloc([128, 8, 64], F32)
        CB = A.alloc([128, 8, 64], BF16)
        BLK = A.alloc([128, 8, 64], F32)
        S.pool("memset", ap=BLK, constant=0.0)
        S.pool("memset", ap=BLK[0:64, 0:4, :], constant=1.0)
        S.pool("memset", ap=BLK[64:128, 4:8, :], constant=1.0)
        ZS = [A.alloc([128, 512], BF16) for _ in range(3)]
        ZT = A.alloc([128, 512], F32)
        XTM = [A.alloc([128, 8, 64], BF16) for _ in range(3)]
        IC = [A.alloc([128, 512], BF16) for _ in range(3)]
        UC = [A.alloc([128, 8, 64], F32) for _ in range(2)]
        BTMs = [A.alloc([128, 128], BF16) for _ in range(2)]
        VHs = [A.alloc([128, 8, 64], BF16) for _ in range(2)]
        VTs = [A.alloc([128, 8, 64], BF16) for _ in range(2)]
        DTts = [[A.alloc([128, 512], BF16) for _ in range(2)] for _ in range(2)]
        STms = [[A.alloc([128, 4, 128], BF16) for _ in range(2)] for _ in range(2)]
        SSBs = [A.alloc([128, 256], BF16) for _ in range(2)]
        ZC = A.alloc([128, 512], BF16)
        YTs = [A.alloc([128, 8, 64], F32) for _ in range(2)]
        YS = A.alloc([128, 512], F32)
        XD = A.alloc([128, 512], BF16)
        SS1 = A.alloc([128, 1], F32)
        YO = [A.alloc([128, 512], BF16) for _ in range(2)]

        def gate_prep(t0, n, d, par):
            proj_fm(ps(0, 0, n, p=8), GWB[:, :, 8 * d:8 * d + 8], 8, t0, n)
            S.act("activation", out=T1[0:8, 0:n], in_=ps(0, 0, n, p=8), func=AF.Exp, bias=GPX[0:8, d, 2:3])
            S.act("activation", out=EL[0:8, 0:n], in_=T1[0:8, 0:n], func=AF.Ln, bias=1.0)
            S.dve("tensor_scalar", out=T1[0:8, 0:n], in0=EL[0:8, 0:n], scalar1=GPX[0:8, d, 3:4], scalar2=None, op0=ALU.mult)
            if d == 0:
                S.dve("tensor_tensor_scan", out=G[0:8, 0:n], data0=SMASK[0:8, 0, 0:n], data1=T1[0:8, 0:n],
                      initial=0.0, op0=ALU.mult, op1=ALU.add)
            else:
                S.dve("tensor_tensor_scan", out=G[0:8, 0:n][:, ::-1], data0=SMASK[0:8, 1, 0:n][:, ::-1],
                      data1=T1[0:8, 0:n][:, ::-1], initial=0.0, op0=ALU.mult, op1=ALU.add)
            split_hilo(G, GHs[par], GLs[par], 8, n)
            TG, DECB = gate_post(G, EL, 8, n, d, GW, par)
            return [TG, DECB, GHs[par], GLs[par], (lambda: gate_post2(EL, 8, n, GW, par))]

        def front(ch, d, final):
            tc, ci, TG, DECB, par, GH, GL, par3 = ch
            BTM, VH, VT, DTt, STm = BTMs[par], VHs[par], VTs[par], DTts[par], STms[par]
            if final:
                proj_tm(ps(2), WZ, 0, 512, tc)
                S.act("activation", out=ZT, in_=ps(2), func=AF.Exp, scale=-1.0)
                S.act("activation", out=ZT, in_=ZT, func=AF.Ln, bias=1.0)
                S.act("activation", out=ZT, in_=ZT, func=AF.Exp, scale=-1.0)
                S.dve("tensor_tensor", out=ZS[par3], in0=ZT, in1=ps(2), op=ALU.mult)
            for i in range(4):
                S.pe("transpose", out=ps_bf(3)[:, i * 128:(i + 1) * 128], in_=XT4[:, i, tc:tc + 128], identity=IDB)
            S.pe("transpose", out=ps_bf(3)[:, 512:640], in_=BT[:, tc:tc + 128], identity=IDB)
            xtm = XTM[par3]
            S.dve("tensor_copy", out=xtm.rearrange("p a b -> p (a b)"), in_=ps_bf(3)[:, 0:512])
            S.dve("tensor_copy", out=BTM, in_=ps_bf(3)[:, 512:640])
            S.pool("tensor_tensor", out=VH, in0=xtm, in1=TG[:, ci, 0, 0:8].unsqueeze(2).to_broadcast([128, 8, 64]), op=ALU.mult)
            S.pool("tensor_tensor", out=VT, in0=VH, in1=TG[:, ci, 1, 0:8].unsqueeze(2).to_broadcast([128, 8, 64]), op=ALU.mult)
            for g in range(2):
                bank = 5 if g == 0 else 4
                for uu in range(4):
                    gdiff(bank, uu, 4 * g + uu, GH, GL, ci * 128, d)
                S.act("activation", out=DTt[g], in_=ps(bank), func=AF.Exp)
            for g in range(2):
                S.pe("matmul", out=ps(1, g * 128, (g + 1) * 128), lhsT=BT[:, tc:tc + 128], rhs=CZ[:, g, tc:tc + 128],
                     start=True, stop=True)
            for g in range(2):
                S.dve("tensor_tensor", out=STm[g], in0=DTt[g].rearrange("p (u t) -> p u t", t=128),
                      in1=ps(1, g * 128, (g + 1) * 128).unsqueeze(1).to_broadcast([128, 4, 128]), op=ALU.mult)
            S.pe("matmul", out=ps(0), lhsT=BTM, rhs=VT.rearrange("p a b -> p (a b)"), start=True, stop=True)
            S.dve("tensor_tensor", out=UC[par], in0=ps(0).rearrange("p (a b) -> p a b", b=64), in1=BLK, op=ALU.mult)
            for e in range(8):
                S.pe("matmul", out=ps(6, e * 64, e * 64 + 64), lhsT=STm[e // 4][:, e % 4, :], rhs=VH[:, e, :],
                     start=True, stop=True)
            S.act("activation", out=IC[par3], in_=ps(6), func=AF.Copy)

        def back(ch, d, final):
            tc, ci, TG, DECB, par, GH, GL, par3 = ch
            YT = YTs[par]
            for g in range(2):
                S.pe("matmul", out=ps(7, g * 256, (g + 1) * 256), lhsT=CZ[:, g, tc:tc + 128],
                     rhs=CB[:, 4 * g:4 * g + 4, :], start=True, stop=True)
            S.dve("tensor_tensor", out=YT, in0=ps(7).rearrange("p (u v) -> p u v", v=64),
                  in1=TG[:, ci, 2, 0:8].unsqueeze(2).to_broadcast([128, 8, 64]), op=ALU.mult)
            S.pool("tensor_tensor", out=CS, in0=CS, in1=DECB[:, 0:8, ci:ci + 1].to_broadcast([128, 8, 64]), op=ALU.mult)
            S.pool("tensor_tensor", out=CS, in0=CS, in1=UC[par], op=ALU.add)
            S.act("activation", out=CB, in_=CS, func=AF.Copy)

        def epi(ch, d, final):
            tc, ci, TG, DECB, par, GH, GL, par3 = ch
            c = tc // 128
            ytf = YTs[par].rearrange("p a b -> p (a b)")
            if not final:
                S.dve("tensor_tensor", out=YB[:, c, :], in0=ytf, in1=IC[par3], op=ALU.add)
            else:
                S.dve("tensor_tensor", out=YS, in0=ytf, in1=IC[par3], op=ALU.add)
                S.dve("tensor_tensor", out=YS, in0=YS, in1=YB[:, c, :], op=ALU.add)
                S.dve("tensor_tensor", out=XD, in0=XTM[par3].rearrange("p a b -> p (a b)"), in1=ROWP[:, 768:1280], op=ALU.mult)
                S.dve("tensor_tensor", out=YS, in0=YS, in1=XD, op=ALU.add)
                S.dve("tensor_tensor", out=YS, in0=YS, in1=ZS[par3], op=ALU.mult)
                S.act("activation", out=XD, in_=YS, func=AF.Square, accum_out=SS1)
                S.act("activation", out=SS1, in_=SS1, func=AF.Ln, scale=1.0 / 512, bias=EPS)
                S.act("activation", out=SS1, in_=SS1, func=AF.Exp, scale=-0.5)
                S.dve("scalar_tensor_tensor", out=YO[par], in0=YS, scalar=SS1[:, 0:1], in1=ROWP[:, 256:768],
                      op0=ALU.mult, op1=ALU.mult)

        def back2(ch, d, final):
            tc, ci, TG, DECB, par, GH, GL, par3 = ch
            if final:
                for i in range(4):
                    S.pe("transpose", out=ps_bf(1)[:, 512 + i * 128:512 + (i + 1) * 128], in_=YO[par][:, i * 128:(i + 1) * 128],
                         identity=IDB)
                S.act("activation", out=Y[:, 2:6, tc:tc + 128],
                      in_=ps_bf(1)[:, 512:1024].rearrange("p (a b) -> p a b", a=4), func=AF.Copy)

        for d in (1, 0):
            final = d == 0
            chk("b_d%d" % d)
            S.pool("memset", ap=CS, constant=0.0)
            S.pool("memset", ap=CB, constant=0.0)
            blist = list(sweep_blocks(d))
            gates = {0: gate_prep(blist[0][0], blist[0][1], d, 0)}
            gates[0][4]()
            q = [None, None, None]
            cnt = 0

            def advance(newch):
                if q[0] is not None:
                    back(q[0], d, final)
                if q[1] is not None:
                    epi(q[1], d, final)
                if q[2] is not None:
                    back2(q[2], d, final)
                q[2] = q[1]
                q[1] = q[0]
                q[0] = newch

            for bidx, (t0, n, cis) in enumerate(blist):
                TG, DECB, GH, GL, _p2 = gates.pop(bidx)
                for k_, ci in enumerate(cis):
                    ch = (t0 + ci * 128, ci, TG, DECB, cnt % 2, GH, GL, cnt % 3)
                    cnt += 1
                    front(ch, d, final)
                    if q[0] is not None:
                        back(q[0], d, final)
                    if k_ == 0 and bidx + 1 < len(blist):
                        gates[bidx + 1] = gate_prep(blist[bidx + 1][0], blist[bidx + 1][1], d, (bidx + 1) % 2)
                    if k_ == 1 and bidx + 1 < len(blist):
                        gates[bidx + 1][4]()
                    if q[1] is not None:
                        epi(q[1], d, final)
                    if q[2] is not None:
                        back2(q[2], d, final)
                    q[2] = q[1]
                    q[1] = q[0]
                    q[0] = ch
            for _ in range(3):
                advance(None)
        A.release(m)

    def phase_out_mlp(l, last):
        m = A.mark()
        XR = A.alloc([128, 8, T], F32)
        RS = A.alloc([128, 512], F32)
        TMP = [A.alloc([128, 512], F32) for _ in range(2)]
        ZB = [A.alloc([128, 512], F32) for _ in range(2)]
        WO = [A.alloc([128, 8, 128], BF16) for _ in range(2)]
        W1 = [A.alloc([128, 8, 128], BF16) for _ in range(3)]
        W2 = [A.alloc([128, 4, D], BF16) for _ in range(2)]
        HS_ = [Y[:, 0:4, :], Y[:, 4:8, :]]
        SQ = Y[:, 0:2, :].rearrange("p a t -> p (a t)")[:, 0:4096].rearrange("p (k n) -> p k n", k=8)
        src = xsrc(l)
        blks = [(t0, n) for (t0, n) in _blocks() if not (last and t0 < TC)]
        for bi, (t0, n) in enumerate(blks):
            S.dma("sp", "xs%d" % (bi % 2), out=XR[:, :, t0:t0 + n], in_=src[:, t0:t0 + n].rearrange("(k p) t -> p k t", p=128))
        chk("m_load")
        cnt = 0
        for i in range(8):
            wo = WO[i % 2]
            load_w(wo, "wo%d" % (i % 2), w_out[l], 128 * i, 128)
            for (t0, n) in blks:
                j = 1 if t0 < TC else 0
                bank = cnt % 4
                cnt += 1
                for k in range(8):
                    S.pe("matmul", out=ps(bank, 0, n), lhsT=wo[:, k, :], rhs=Y[:, k, t0:t0 + n], start=(k == 0), stop=(k == 7))
                S.dve("scalar_tensor_tensor", out=XR[:, i, t0:t0 + n], in0=ps(bank, 0, n), scalar=MOD[:, 16 + i, j:j + 1],
                      in1=XR[:, i, t0:t0 + n], op0=ALU.mult, op1=ALU.add)
        chk("m_wout")
        for (t0, n) in blks:
            j = 1 if t0 < TC else 0
            S.act("activation", out=SQ[:, :, 0:n], in_=XR[:, :, t0:t0 + n], func=AF.Square)
            for k in range(8):
                S.pe("matmul", out=ps(4, 0, n), lhsT=ONESB, rhs=SQ[:, k, 0:n], start=(k == 0), stop=(k == 7))
            S.act("activation", out=RS[:, 0:n], in_=ps(4, 0, n), func=AF.Ln, scale=1.0 / D, bias=EPS)
            S.act("activation", out=RS[:, 0:n], in_=RS[:, 0:n], func=AF.Exp, scale=-0.5)
            for k in range(8):
                tmp = TMP[k % 2]
                S.dve("tensor_tensor", out=tmp[:, 0:n], in0=XR[:, k, t0:t0 + n], in1=RS[:, 0:n], op=ALU.mult)
                S.act("activation", out=XN[:, k, t0:t0 + n], in_=tmp[:, 0:n], func=AF.Identity,
                      scale=A2[:, k, j:j + 1], bias=MOD[:, 24 + k, j:j + 1])
        chk("m_norm")
        c1 = 0
        c2 = 0
        for hg in range(8):
            Hg = HS_[hg % 2]
            w2 = W2[hg % 2]
            for hf in range(2):
                S.dma("pool", "w2%d%d" % (hg % 2, hf), out=w2[:, :, 512 * hf:512 * hf + 512],
                      in_=w_mlp2[l][512 * hg:512 * hg + 512, 512 * hf:512 * hf + 512].rearrange("(j p) n -> p j n", p=128))
            for jj in range(4):
                jh = 4 * hg + jj
                w1 = W1[jh % 3]
                load_w(w1, "w1%d" % (jh % 3), w_mlp1[l], 128 * jh, 128)
                for (t0, n) in blks:
                    bank = c1 % 4
                    c1 += 1
                    for k in range(8):
                        S.pe("matmul", out=ps(bank, 0, n), lhsT=w1[:, k, :], rhs=XN[:, k, t0:t0 + n], start=(k == 0), stop=(k == 7))
                    zb = ZB[c1 % 2]
                    S.dve("tensor_scalar", out=zb[:, 0:n], in0=ps(bank, 0, n), scalar1=pc(l, "b_mlp1", jh), scalar2=0.0,
                          op0=ALU.add, op1=ALU.max)
                    S.act("activation", out=Hg[:, jj, t0:t0 + n], in_=zb[:, 0:n], func=AF.Square)
            chk("m_h%d" % hg)
            for i in range(8):
                for (t0, n) in blks:
                    j = 1 if t0 < TC else 0
                    bank = 4 + c2 % 4
                    c2 += 1
                    for jj in range(4):
                        S.pe("matmul", out=ps(bank, 0, n), lhsT=w2[:, jj, 128 * i:128 * i + 128], rhs=Hg[:, jj, t0:t0 + n],
                             start=(jj == 0), stop=(jj == 3))
                    S.dve("scalar_tensor_tensor", out=XR[:, i, t0:t0 + n], in0=ps(bank, 0, n), scalar=MOD[:, 40 + i, j:j + 1],
                          in1=XR[:, i, t0:t0 + n], op0=ALU.mult, op1=ALU.add)
        chk("m_mlp")
        for i in range(8):
            for (ta, tb, j) in ((0, TC, 1), (TC, T, 0)):
                if last and j == 1:
                    continue
                S.act("activation", out=XR[:, i, ta:tb], in_=XR[:, i, ta:tb], func=AF.Identity, bias=GB2[:, i, j:j + 1])
        chk("m_b2")
        if not last:
            for bi, (t0, n) in enumerate(blks):
                S.dma("sp", "xo%d" % bi, out=xscr[:, t0:t0 + n].rearrange("(k p) t -> p k t", p=128), in_=XR[:, :, t0:t0 + n])
        else:
            for bi, (t0, n) in enumerate(blks):
                S.act("activation", out=SQ[:, :, 0:n], in_=XR[:, :, t0:t0 + n], func=AF.Square)
                for k in range(8):
                    S.pe("matmul", out=ps(4, 0, n), lhsT=ONESB, rhs=SQ[:, k, 0:n], start=(k == 0), stop=(k == 7))
                S.act("activation", out=RS[:, 0:n], in_=ps(4, 0, n), func=AF.Ln, scale=1.0 / D, bias=EPS)
                S.act("activation", out=RS[:, 0:n], in_=RS[:, 0:n], func=AF.Exp, scale=-0.5)
                for k in range(8):
                    S.dve("scalar_tensor_tensor", out=XR[:, k, t0:t0 + n], in0=XR[:, k, t0:t0 + n],
                          scalar=PC[:, PC_GFINAL + k:PC_GFINAL + k + 1], in1=RS[:, 0:n], op0=ALU.mult, op1=ALU.mult)
                chk("m_fin%d" % bi)
                ch = "xo%d" % bi
                S.dma("sp", ch, out=outT[:, t0 - TC:t0 - TC + n].rearrange("(k p) t -> p k t", p=128), in_=XR[:, :, t0:t0 + n])
                if ch not in final_chans:
                    final_chans.append(ch)
        A.release(m)

    try:
      for l in range(nlayers):
        _mm = A.mark()
        S.tag = "norm%d" % l
        _XS, _RSA = phase_norm_stats(l, xsrc(l))
        S.tag = "mod%d" % l
        phase_mod(l)
        if upto == "mod":
            break
        S.tag = "norm%d" % l
        phase_norm_apply(l, xsrc(l), 0, A1, _XS, _RSA)
        A.release(_mm)
        if upto == "norm1":
            break
        S.tag = "mixc%d" % l
        if "noc" not in taps:
            mixer_c(l)
        if upto == "mixc":
            break
        S.tag = "mixa%d" % l
        if "noa" not in taps:
            mixer_a(l)
        if upto == "mixa":
            break
        S.tag = "mixb%d" % l
        mixer_b(l)
        if upto == "mixb":
            break
        S.tag = "mlp%d" % l
        phase_out_mlp(l, l == nlayers - 1)
    except _Stop:
        pass
    tap("MOD", MOD)
    tap("XN", XN)
    tap("Y", Y)
    if "XS" in taps:
        pass
    if not final_chans:
        raise RuntimeError("no outputs")
    import os as _os
    if _os.environ.get("KTAGMAP"):
        S.tagmap = {}
    S.emit(stack, final_chans)
    if S.tagmap is not None:
        import json as _json
        _json.dump(S.tagmap, open(_os.environ["KTAGMAP"], "w"))
    stack.close()
    return nc, tap_out


def _pack_inputs(inputs):
    f = lambda a: np.ascontiguousarray(np.asarray(a, dtype=np.float32))
    pcols = np.zeros((NPC, 128), np.float32)
    for l in range(DEPTH):
        b = l * PC_PER_LAYER

        def put(name, arr):
            arr = f(arr).reshape(-1, 128)
            o = b + _PC_OFF[name]
            pcols[o:o + arr.shape[0]] = arr

        put("g_mix", inputs["g_mix"][l])
        put("g_mlp", inputs["g_mlp"][l])
        put("b_ada", inputs["b_ada"][l])
        put("b_mlp1", inputs["b_mlp1"][l])
        put("b_mlp2", inputs["b_mlp2"][l])
        put("conv_a_w", inputs["conv_a_w"][l])
        put("conv_a_b", inputs["conv_a_b"][l])
        put("conv_b_w", inputs["conv_b_w"][l])
        put("conv_b_b", inputs["conv_b_b"][l])
        put("conv_c_w", inputs["conv_c_w"][l])
        put("conv_c_b", inputs["conv_c_b"][l])
        put("b_rg", inputs["b_rg"][l])
        put("lam", inputs["lam"][l])
    pcols[PC_GFINAL:PC_GFINAL + 8] = f(inputs["g_final"]).reshape(8, 128)
    consts = np.zeros((128, NCONST), np.float32)
    consts[:, C_ID:C_ID + 128] = np.eye(128, dtype=np.float32)
    s = np.arange(128)[:, None]
    t = np.arange(128)[None, :]
    consts[:, C_MF:C_MF + 128] = np.where(s <= t, 0.0, -30000.0)
    consts[:, C_MB:C_MB + 128] = np.where(s >= t, 0.0, -30000.0)
    for u in range(8):
        consts[u, C_SEL + u * 128:C_SEL + (u + 1) * 128] = 1.0
        consts[u, C_NSEL + u * 128:C_NSEL + (u + 1) * 128] = -1.0
    consts[0:8, C_GSEL:C_GSEL + 128] = 1.0
    gpar = np.zeros((DEPTH, 2, 8, 4), np.float32)
    gpar[:, :, 0:4, 0] = f(inputs["b_ig"])
    gpar[:, :, 0:4, 1] = f(inputs["b_fg"])
    gpar[:, :, :, 2] = f(inputs["dt_bias"])
    gpar[:, :, :, 3] = f(inputs["a_log"])
    rowp = np.zeros((DEPTH, 1280), np.float32)
    rowp[:, 0:256] = f(inputs["g_head_a"]).reshape(DEPTH, 256)
    rowp[:, 256:768] = f(inputs["g_norm_b"])
    rowp[:, 768:1280] = np.repeat(f(inputs["d_skip"]), 64, axis=1)
    shared = {"pcols": pcols, "consts": consts, "gpar": gpar, "rowp": rowp,
              "w_ada": f(inputs["w_ada"]), "w_in": f(inputs["w_in"]), "w_out": f(inputs["w_out"]),
              "w_mlp1": f(inputs["w_mlp1"]), "w_mlp2": f(inputs["w_mlp2"]), "w_rg": f(inputs["w_rg"])}
    x = f(inputs["x"])
    ctx = f(inputs["ctx"])
    c = f(inputs["c"])
    c_ctx = f(inputs["c_ctx"])
    maps = []
    for b in range(x.shape[0]):
        xT = np.ascontiguousarray(np.concatenate([ctx[b], x[b]], axis=0).T)
        cv = np.stack([c[b], c_ctx], axis=0)
        cvT = np.ascontiguousarray(cv.reshape(2, 8, 128).transpose(2, 1, 0))
        mp = dict(shared)
        mp["xT"] = xT
        mp["cvT"] = cvT
        maps.append(mp)
    return maps


def kernel(**inputs):
    maps = _pack_inputs(inputs)
    nc, _ = build()
    res = run_bass_kernel_spmd(nc, maps, core_ids=list(range(8)))
    out = np.stack([np.ascontiguousarray(r["outT"].T) for r in res.results], axis=0)
    return out.astype(np.float32)
```

```python
import math
import numpy as np
import concourse.bass as bass
import concourse.mybir as mybir
from concourse.bass_utils import run_bass_kernel_spmd

F32 = mybir.dt.float32
BF16 = mybir.dt.bfloat16
ALU = mybir.AluOpType
AF = mybir.ActivationFunctionType
AX = mybir.AxisListType

D = 1024
TC = 256
TL = 2048
T = TC + TL
DEPTH = 2
P_IN = 2848
EPS = 1e-6
NPC = 384
LN8 = math.log(0.125)
GELU_C = 2.0 * math.sqrt(2.0 / math.pi)
STRICT_SAME_ENGINE = True

_PC_LAYOUT = [("g_mix", 8), ("g_mlp", 8), ("b_ada", 48), ("b_mlp1", 32), ("b_mlp2", 8),
              ("conv_a_w", 16), ("conv_a_b", 4), ("conv_b_w", 24), ("conv_b_b", 6),
              ("conv_c_w", 8), ("conv_c_b", 2), ("b_rg", 8), ("lam", 4)]
_PC_OFF = {}
_o = 0
for _n, _c in _PC_LAYOUT:
    _PC_OFF[_n] = _o
    _o += _c
PC_PER_LAYER = _o
PC_GFINAL = DEPTH * PC_PER_LAYER

C_ID = 0
C_MF = 128
C_MB = 256
C_GSEL = 384
NCMAIN = 512
C_SEL = 512
C_NSEL = 1536
NCONST = 2560


def _dsize(dt):
    s = str(dt)
    if "bfloat16" in s or "float16" in s:
        return 2
    if "8" in s and "float" in s:
        return 1
    return 4


def _prod(xs):
    r = 1
    for v in xs:
        r *= int(v)
    return r


class _Ins:
    __slots__ = ("stream", "fn", "deps", "dma", "chan", "cval", "inc", "val", "tag")

    def __init__(self, stream, fn, dma=False, chan=None, cval=0):
        self.stream = stream
        self.fn = fn
        self.deps = []
        self.dma = dma
        self.chan = chan
        self.cval = cval
        self.inc = False
        self.val = 0
        self.tag = ""


class Sched:
    STREAMS = ("pe", "act", "dve", "pool", "sp")
    ATTR = {"pe": "tensor", "act": "scalar", "dve": "vector", "pool": "gpsimd", "sp": "sync"}

    def __init__(self, nc):
        self.nc = nc
        self.streams = {s: [] for s in self.STREAMS}
        self.recs = {}
        self.chan_cnt = {}
        self.chan_last = {}
        self.tag = ""
        self.tagmap = None
        self.mloc = {}
        self.n = 0

    def region(self, ap):
        sp = str(ap.space)
        t = ap.tensor
        name = t.name
        steps = ap.ap
        off = ap.offset
        ds = _dsize(ap.dtype)
        if "DRAM" in sp:
            lo = hi = off
            for st, cnt in steps:
                e = st * (cnt - 1)
                if e > 0:
                    hi += e
                else:
                    lo += e
            return (("D", name), 0, 1, lo * ds, (hi + 1) * ds)
        ps = _prod(t.shape[1:])
        p0 = off // ps
        f0 = off - p0 * ps
        pst, pc = steps[0]
        if pc > 1:
            assert pst % ps == 0, (name, steps)
            p1 = p0 + (pc - 1) * (pst // ps) + 1
        else:
            p1 = p0 + 1
        lo = hi = f0
        for st, cnt in steps[1:]:
            e = st * (cnt - 1)
            if e > 0:
                hi += e
            else:
                lo += e
        m = self.mloc.get(name)
        if m is None:
            ml = self.nc.lookup_mloc(name)
            m = (int(ml.addr), int(ml.bank))
            self.mloc[name] = m
        if "PSUM" in sp:
            key = ("P", m[1])
        else:
            key = ("S",)
        return (key, p0, p1, m[0] + lo * ds, m[0] + (hi + 1) * ds)

    def _add_dep(self, ins, d, raw):
        if d is ins:
            return
        if d.stream == ins.stream and not d.dma and not ins.dma:
            if ins.stream == "pe":
                return
            if not raw and not STRICT_SAME_ENGINE:
                return
        ins.deps.append(d)

    def access(self, ins, ap, write):
        key, p0, p1, lo, hi = self.region(ap)
        if key[0] == "P" and ins.stream == "pe" and write:
            p0 = (p0 // 32) * 32
            p1 = ((p1 + 31) // 32) * 32
            lo, hi = 0, 2048
        L = self.recs.get(key)
        if L is None:
            L = []
            self.recs[key] = L
        newL = []
        for r in L:
            rp0, rp1, rlo, rhi, rins, rw = r
            if rp0 < p1 and p0 < rp1 and rlo < hi and lo < rhi:
                if write or rw:
                    self._add_dep(ins, rins, raw=(rw and not write))
                pcov = p0 <= rp0 and rp1 <= p1
                if write and pcov:
                    if rlo < lo:
                        newL.append((rp0, rp1, rlo, lo, rins, rw))
                    if hi < rhi:
                        newL.append((rp0, rp1, hi, rhi, rins, rw))
                    continue
                if (not write) and (not rw) and pcov and lo <= rlo and rhi <= hi \
                        and rins.stream == ins.stream and not rins.dma and not ins.dma:
                    continue
            newL.append(r)
        newL.append((p0, p1, lo, hi, ins, write))
        self.recs[key] = newL

    def op(self, stream, opname, *, extra_r=(), extra_w=(), **kw):
        nm = opname

        def fn(eng, nm=nm, kw=kw):
            return getattr(eng, nm)(**kw)

        ins = _Ins(stream, fn)
        ins.tag = self.tag
        reads, writes = [], []
        for k, v in kw.items():
            if type(v).__name__ != "AP":
                continue
            if k in ("out", "accum_out") or (k == "ap" and nm in ("memset", "memzero")):
                writes.append(v)
            else:
                reads.append(v)
        for v in list(reads) + list(extra_r):
            self.access(ins, v, False)
        for v in list(writes) + list(extra_w):
            self.access(ins, v, True)
        self.streams[stream].append(ins)
        self.n += 1
        return ins

    def dma(self, stream, chan, out, in_, **kw):
        cnt = self.chan_cnt.get(chan, 0) + 1
        self.chan_cnt[chan] = cnt

        def fn(eng, out=out, in_=in_, kw=kw):
            return eng.dma_start(out=out, in_=in_, **kw)

        ins = _Ins(stream, fn, dma=True, chan=chan, cval=16 * cnt)
        ins.tag = self.tag
        prev = self.chan_last.get(chan)
        if prev is not None and not chan.startswith("par"):
            ins.deps.append(prev)
        self.chan_last[chan] = ins
        self.access(ins, in_, False)
        self.access(ins, out, True)
        self.streams[stream].append(ins)
        self.n += 1
        return ins

    def pe(self, opname, **kw):
        return self.op("pe", opname, **kw)

    def act(self, opname, **kw):
        return self.op("act", opname, **kw)

    def dve(self, opname, **kw):
        return self.op("dve", opname, **kw)

    def pool(self, opname, **kw):
        return self.op("pool", opname, **kw)

    def emit(self, stack, final_chans):
        nc = self.nc
        for s in self.STREAMS:
            for ins in self.streams[s]:
                for d in ins.deps:
                    if not d.dma:
                        d.inc = True
        for s in self.STREAMS:
            c = 0
            for ins in self.streams[s]:
                if ins.inc and not ins.dma:
                    c += 1
                    ins.val = c
        esem = {s: stack.enter_context(nc.semaphore("e_" + s)) for s in ("pe", "act", "dve", "pool")}
        csem = {c: stack.enter_context(nc.semaphore("c_" + c)) for c in self.chan_cnt}
        block = stack.enter_context(nc.Block())
        sched = self

        def run(stream, eng):
            known = {}
            for ins in sched.streams[stream]:
                need = {}
                for d in ins.deps:
                    if d.dma:
                        k, v = ("c", d.chan), d.cval
                        if d.chan.startswith("par"):
                            v = 16 * sched.chan_cnt[d.chan]
                    else:
                        k, v = ("e", d.stream), d.val
                    if v > known.get(k, 0) and v > need.get(k, 0):
                        need[k] = v
                for k, v in need.items():
                    sem = csem[k[1]] if k[0] == "c" else esem[k[1]]
                    eng.wait_ge(sem, v)
                    known[k] = v
                bi = ins.fn(eng)
                if sched.tagmap is not None:
                    try:
                        sched.tagmap[str(bi.ins.name)] = ins.tag
                    except Exception:
                        pass
                if ins.dma:
                    bi.then_inc(csem[ins.chan], 16)
                elif ins.inc:
                    bi.then_inc(esem[stream], 1)
            if stream == "sp":
                for c in final_chans:
                    eng.wait_ge(csem[c], 16 * sched.chan_cnt[c])

        @block.tensor
        def _(e):
            run("pe", e)

        @block.scalar
        def _(e):
            run("act", e)

        @block.vector
        def _(e):
            run("dve", e)

        @block.gpsimd
        def _(e):
            run("pool", e)

        @block.sync
        def _(e):
            run("sp", e)


class Arena:
    def __init__(self, nc, stack, nbytes):
        self.nw = nbytes // 4
        self.t = stack.enter_context(nc.sbuf_tensor("arena", [128, self.nw], F32))
        self.top = 0
        self.peak = 0

    def alloc(self, shape, dtype):
        ds = 2 if dtype == BF16 else 4
        nb = _prod(shape[1:]) * ds
        nbr = (nb + 63) // 64 * 64
        lo = self.top
        self.top += nbr
        self.peak = max(self.peak, self.top)
        assert self.top <= self.nw * 4, ("SBUF arena overflow", self.top, self.nw * 4)
        v = self.t[:, lo // 4:(lo + nb + 3) // 4]
        if dtype == BF16:
            v = v.bitcast(BF16)
        n = _prod(shape[1:])
        v = v[:, 0:n]
        if len(shape) == 3:
            v = v.rearrange("p (a b) -> p a b", a=shape[1], b=shape[2])
        elif len(shape) == 4:
            v = v.rearrange("p (a b c) -> p a b c", a=shape[1], b=shape[2], c=shape[3])
        if shape[0] != 128:
            v = v[0:shape[0]]
        return v

    def mark(self):
        return self.top

    def release(self, m):
        self.top = m


def _blocks():
    return [(0, 256)] + [(256 + 512 * i, 512) for i in range(4)]


class _Stop(Exception):
    pass


def build(nlayers=DEPTH, upto=None, taps=(), stop_at=None):
    from contextlib import ExitStack

    def chk(name):
        if stop_at == name:
            raise _Stop()
    nc = bass.Bass("TRN2", target_bir_lowering=False)
    stack = ExitStack()
    S = Sched(nc)

    def din(name, shape, dt=F32):
        return nc.dram_tensor(name, list(shape), dt, kind="ExternalInput").ap()

    xT_in = din("xT", [D, T])
    cvT_in = din("cvT", [128, 8, 2])
    pcols_in = din("pcols", [NPC, 128])
    consts_in = din("consts", [128, NCONST])
    gpar_in = din("gpar", [DEPTH, 2, 8, 4])
    rowp_in = din("rowp", [DEPTH, 1280])
    w_ada = din("w_ada", [DEPTH, D, 6 * D])
    w_in = din("w_in", [DEPTH, D, P_IN])
    w_out = din("w_out", [DEPTH, D, D])
    w_mlp1 = din("w_mlp1", [DEPTH, D, 4 * D])
    w_mlp2 = din("w_mlp2", [DEPTH, 4 * D, D])
    w_rg = din("w_rg", [DEPTH, 2, 2, 4, 64, 64])
    outT = nc.dram_tensor("outT", [D, TL], F32, kind="ExternalOutput").ap()
    xscr = nc.dram_tensor("xscr", [D, T], F32, kind="Internal").ap()
    tap_out = {}
    final_chans = []

    A = Arena(nc, stack, 211968)
    PS = [stack.enter_context(nc.psum_tensor("psb%d" % i, [128, 512], F32)) for i in range(8)]

    def ps(i, lo=0, hi=512, p=128):
        return PS[i][0:p, lo:hi]

    def ps_bf(i):
        return PS[i][:, :].bitcast(BF16)

    def tap(name, ap, stream="sp"):
        if name not in taps:
            return
        shp = list(ap.shape)
        dt_ = ap.dtype
        t = nc.dram_tensor("tap_" + name, shp, dt_, kind="ExternalOutput").ap()
        tap_out[name] = (shp, dt_)
        ch = "tap_" + name
        S.dma(stream, ch, out=t, in_=ap)
        final_chans.append(ch)

    CONST = A.alloc([128, NCMAIN], F32)
    PC = A.alloc([128, NPC], F32)
    XN = A.alloc([128, 8, T], BF16)
    Y = A.alloc([128, 8, T], BF16)
    IDB = A.alloc([128, 128], BF16)
    ONESB = A.alloc([128, 128], BF16)
    MADD = A.alloc([128, 2, 128], BF16)
    CSS = A.alloc([128, 8, 2], F32)
    CSSB = A.alloc([128, 8, 2], BF16)
    MOD = A.alloc([128, 48, 2], F32)
    A1 = A.alloc([128, 8, 2], F32)
    A2 = A.alloc([128, 8, 2], F32)
    GB2 = A.alloc([128, 8, 2], F32)
    ROWP = A.alloc([128, 1280], F32)
    GP = A.alloc([8, 2, 4], F32)
    GPX = A.alloc([8, 2, 4], F32)
    CL = A.alloc([128, 4, 2], F32)
    SMASK = A.alloc([8, 2, 512], F32)
    IDENT = CONST[:, C_ID:C_ID + 128]
    SELB = A.alloc([128, 2048], BF16)

    S.dma("sp", "par", out=CONST, in_=consts_in[:, 0:NCMAIN])
    _m0 = A.mark()
    PCT = A.alloc([128, 3, 128], F32)
    SELF = A.alloc([128, 2048], F32)
    A.release(_m0)
    S.dma("sp", "par", out=SELF, in_=consts_in[:, C_SEL:C_SEL + 2048])
    S.dma("sp", "par", out=PCT, in_=pcols_in.rearrange("(a p) c -> p a c", p=128))
    S.dma("sp", "par", out=CSS, in_=cvT_in)
    for a in range(3):
        S.pe("transpose", out=ps(0, a * 128, a * 128 + 128), in_=PCT[:, a, :], identity=IDENT)
    S.dve("tensor_copy", out=PC, in_=ps(0, 0, 384))
    S.dve("tensor_copy", out=IDB, in_=IDENT)
    S.dve("memset", ap=ONESB, constant=1.0)
    S.dve("tensor_copy", out=SELB, in_=SELF)
    S.dve("tensor_copy", out=MADD, in_=CONST[:, C_MF:C_MF + 256].rearrange("p (a b) -> p a b", a=2))
    S.act("activation", out=CSS, in_=CSS, func=AF.Silu)
    S.dve("tensor_copy", out=CSSB, in_=CSS)
    S.dve("memset", ap=SMASK, constant=1.0)
    S.dve("memset", ap=SMASK[:, 0, :].rearrange("p (c l) -> p c l", l=128)[:, :, 0:1], constant=0.0)
    S.dve("memset", ap=SMASK[:, 1, :].rearrange("p (c l) -> p c l", l=128)[:, :, 127:128], constant=0.0)

    def pc(l, name, idx=0, n=1):
        o = l * PC_PER_LAYER + _PC_OFF[name] + idx
        return PC[:, o:o + n]

    base_mark = A.mark()
    if "zy" in taps:
        S.pool("memset", ap=Y, constant=0.0)

    def phase_mod(l):
        WA = [A.alloc([128, 8, 512], BF16) for _ in range(3)]
        MODR = A.alloc([2, 6 * D], F32)
        for jg in range(12):
            slot = WA[jg % 3]
            S.dma("pool", "wa%d" % (jg % 3), out=slot,
                  in_=w_ada[l][:, jg * 512:(jg + 1) * 512].rearrange("(k p) n -> p k n", p=128))
            bank = 2 + jg % 4
            for k in range(8):
                S.pe("matmul", out=ps(bank, 0, 512, p=2), lhsT=CSSB[:, k, :], rhs=slot[:, k, :], start=(k == 0), stop=(k == 7))
            S.act("activation", out=MODR[:, jg * 512:(jg + 1) * 512], in_=ps(bank, 0, 512, p=2), func=AF.Copy)
        for j in range(48):
            S.pe("matmul", out=ps(1, 2 * j, 2 * j + 2), lhsT=MODR[0:2, j * 128:(j + 1) * 128], rhs=IDENT[0:2, 0:2],
                 start=True, stop=True)
        S.dve("tensor_tensor", out=MOD, in0=ps(1, 0, 96).rearrange("p (a b) -> p a b", b=2),
              in1=pc(l, "b_ada", 0, 48).unsqueeze(2).to_broadcast([128, 48, 2]), op=ALU.add)
        S.dve("tensor_scalar", out=A1, in0=MOD[:, 8:16, :], scalar1=1.0, scalar2=None, op0=ALU.add)
        S.dve("tensor_tensor", out=A1, in0=A1, in1=pc(l, "g_mix", 0, 8).unsqueeze(2).to_broadcast([128, 8, 2]),
              op=ALU.mult)
        S.dve("tensor_scalar", out=A2, in0=MOD[:, 32:40, :], scalar1=1.0, scalar2=None, op0=ALU.add)
        S.dve("tensor_tensor", out=A2, in0=A2, in1=pc(l, "g_mlp", 0, 8).unsqueeze(2).to_broadcast([128, 8, 2]),
              op=ALU.mult)
        S.dve("tensor_tensor", out=GB2, in0=MOD[:, 40:48, :],
              in1=pc(l, "b_mlp2", 0, 8).unsqueeze(2).to_broadcast([128, 8, 2]), op=ALU.mult)
        S.dma("sp", "par_m%d" % l, out=ROWP, in_=rowp_in[l:l + 1, :].partition_broadcast(128))
        S.dma("sp", "par_m%d" % l, out=GP, in_=gpar_in[l].rearrange("d r k -> r d k"))
        S.dve("tensor_scalar", out=GPX[:, :, 0:1], in0=GP[:, :, 0:1], scalar1=LN8, scalar2=None, op0=ALU.add)
        S.dve("tensor_scalar", out=GPX[:, :, 1:2], in0=GP[:, :, 1:2], scalar1=-1.0, scalar2=None, op0=ALU.mult)
        S.dve("tensor_copy", out=GPX[:, :, 2:3], in_=GP[:, :, 2:3])
        S.act("activation", out=GPX[:, :, 3:4], in_=GP[:, :, 3:4], func=AF.Exp)
        S.dve("tensor_scalar", out=GPX[:, :, 3:4], in0=GPX[:, :, 3:4], scalar1=-1.0, scalar2=None, op0=ALU.mult)
        S.act("activation", out=CL[:, :, 0], in_=pc(l, "lam", 0, 4), func=AF.Exp, scale=-1.0)
        S.act("activation", out=CL[:, :, 0], in_=CL[:, :, 0], func=AF.Ln, bias=1.0)
        S.dve("tensor_scalar", out=CL[:, :, 1], in0=CL[:, :, 0], scalar1=-16.0, scalar2=None, op0=ALU.mult)
        S.dve("tensor_scalar", out=CL[:, :, 0], in0=CL[:, :, 0], scalar1=-8.0, scalar2=None, op0=ALU.mult)

    def xsrc(l):
        return xT_in if l == 0 else xscr

    def phase_norm_stats(l, src):
        XS = [A.alloc([128, 8, 512], F32) for _ in range(2)]
        SQ = A.alloc([128, 8, 512], BF16)
        RSA = A.alloc([128, T], F32)
        for bi, (t0, n) in enumerate(_blocks()):
            xs = XS[bi % 2]
            S.dma("sp", "xs%d" % (bi % 2), out=xs[:, :, 0:n], in_=src[:, t0:t0 + n].rearrange("(k p) t -> p k t", p=128))
            S.act("activation", out=SQ[:, :, 0:n], in_=xs[:, :, 0:n], func=AF.Square)
            for k in range(8):
                S.pe("matmul", out=ps(0, 0, n), lhsT=ONESB, rhs=SQ[:, k, 0:n], start=(k == 0), stop=(k == 7))
            S.act("activation", out=RSA[:, t0:t0 + n], in_=ps(0, 0, n), func=AF.Ln, scale=1.0 / D, bias=EPS)
            S.act("activation", out=RSA[:, t0:t0 + n], in_=RSA[:, t0:t0 + n], func=AF.Exp, scale=-0.5)
        return XS, RSA

    def phase_norm_apply(l, src, which, amod, XS, RSA):
        TMP = A.alloc([128, 8, 512], F32)
        for bi, (t0, n) in enumerate(_blocks()):
            j = 1 if t0 < TC else 0
            xs = XS[bi % 2]
            S.dma("sp", "xs%d" % (bi % 2), out=xs[:, :, 0:n], in_=src[:, t0:t0 + n].rearrange("(k p) t -> p k t", p=128))
            S.dve("tensor_tensor", out=TMP[:, :, 0:n], in0=xs[:, :, 0:n],
                  in1=RSA[:, t0:t0 + n].unsqueeze(1).to_broadcast([128, 8, n]), op=ALU.mult)
            for k in range(8):
                S.act("activation", out=XN[:, k, t0:t0 + n], in_=TMP[:, k, 0:n], func=AF.Identity,
                      scale=amod[:, k, j:j + 1], bias=MOD[:, which * 8 + k, j:j + 1])

    def phase_norm(l, src, which, amod):
        m = A.mark()
        XS = [A.alloc([128, 8, 512], F32) for _ in range(2)]
        SQ = A.alloc([128, 8, 512], BF16)
        RS = A.alloc([128, 512], F32)
        TMP = A.alloc([128, 8, 512], F32)
        for bi, (t0, n) in enumerate(_blocks()):
            j = 1 if t0 < TC else 0
            xs = XS[bi % 2]
            S.dma("sp", "xs%d" % (bi % 2), out=xs[:, :, 0:n], in_=src[:, t0:t0 + n].rearrange("(k p) t -> p k t", p=128))
            S.act("activation", out=SQ[:, :, 0:n], in_=xs[:, :, 0:n], func=AF.Square)
            for k in range(8):
                S.pe("matmul", out=ps(0, 0, n), lhsT=ONESB, rhs=SQ[:, k, 0:n], start=(k == 0), stop=(k == 7))
            S.act("activation", out=RS[:, 0:n], in_=ps(0, 0, n), func=AF.Ln, scale=1.0 / D, bias=EPS)
            S.act("activation", out=RS[:, 0:n], in_=RS[:, 0:n], func=AF.Exp, scale=-0.5)
            S.dve("tensor_tensor", out=TMP[:, :, 0:n], in0=xs[:, :, 0:n],
                  in1=RS[:, 0:n].unsqueeze(1).to_broadcast([128, 8, n]), op=ALU.mult)
            for k in range(8):
                S.act("activation", out=XN[:, k, t0:t0 + n], in_=TMP[:, k, 0:n], func=AF.Identity,
                      scale=amod[:, k, j:j + 1], bias=MOD[:, which * 8 + k, j:j + 1])
        A.release(m)

    def load_w(slot, chan, src2d, c0, ncols):
        S.dma("pool", chan, out=slot[:, :, 0:ncols],
              in_=src2d[:, c0:c0 + ncols].rearrange("(k p) n -> p k n", p=128))

    def proj_fm(psum_ap, wslot, M, t0, n):
        for k in range(8):
            S.pe("matmul", out=psum_ap, lhsT=wslot[:, k, 0:M], rhs=XN[:, k, t0:t0 + n],
                 start=(k == 0), stop=(k == 7))

    def proj_tm(psum_ap, wslot, c0, ncols, t0):
        for k in range(8):
            S.pe("matmul", out=psum_ap, lhsT=XN[:, k, t0:t0 + 128], rhs=wslot[:, k, c0:c0 + ncols],
                 start=(k == 0), stop=(k == 7))

    def conv_tile(PRE, OUT, wcol, bcol):
        for (o0, n, p0) in ((0, TC, 0), (TC, TL, 259)):
            S.dve("tensor_scalar", out=OUT[:, o0:o0 + n], in0=PRE[:, p0:p0 + n], scalar1=wcol(0), scalar2=bcol,
                  op0=ALU.mult, op1=ALU.add)
            for j in range(1, 4):
                S.dve("scalar_tensor_tensor", out=OUT[:, o0:o0 + n], in0=PRE[:, p0 + j:p0 + j + n], scalar=wcol(j),
                      in1=OUT[:, o0:o0 + n], op0=ALU.mult, op1=ALU.add)

    def zero_halo(PRE):
        S.pool("memset", ap=PRE[:, 0:2], constant=0.0)
        S.pool("memset", ap=PRE[:, 258:261], constant=0.0)
        S.pool("memset", ap=PRE[:, 2309:2312], constant=0.0)

    def mixer_c(l):
        m = A.mark()
        WBD = A.alloc([128, 8, 128], F32)
        S.pool("memset", ap=WBD, constant=0.0)
        for d in range(2):
            for g in range(2):
                for nb in range(4):
                    j, hb = nb // 2, nb % 2
                    S.dma("sp", "par_w%d" % l, out=WBD[64 * hb:64 * hb + 64, d * 4 + g * 2 + j, 64 * hb:64 * hb + 64],
                          in_=w_rg[l, d, g, nb])
        WS = [A.alloc([128, 8, 128], BF16) for _ in range(2)]
        PRE = A.alloc([128, 2312], F32)
        GR = A.alloc([128, T], F32)
        XC = A.alloc([128, T], F32)
        HS = A.alloc([128, T], F32)
        AA = A.alloc([128, T], F32)
        UU = A.alloc([128, T], F32)
        RGf = A.alloc([128, T], F32)
        IGf = A.alloc([128, T], F32)
        H1 = PRE[:, 0:T]
        for j in range(2):
            load_w(WS[0], "wc0", w_in[l], 2336 + 128 * j, 128)
            load_w(WS[1], "wc1", w_in[l], 2592 + 128 * j, 128)
            zero_halo(PRE)
            for (t0, n) in _blocks():
                proj_fm(ps(2, 0, n), WS[0], 128, t0, n)
                if t0 < TC:
                    S.act("activation", out=PRE[:, 2:2 + TC], in_=ps(2, 0, n), func=AF.Copy)
                else:
                    r0 = (t0 - TC) // 64
                    S.act("activation",
                          out=PRE[:, 261:261 + TL].rearrange("p (w r) -> p w r", r=32)[:, :, r0:r0 + 8],
                          in_=ps(2, 0, 512).rearrange("p (r w) -> p w r", w=64), func=AF.Copy)
                proj_fm(ps(3, 0, n), WS[1], 128, t0, n)
                S.dve("tensor_copy", out=GR[:, t0:t0 + n], in_=ps(3, 0, n))
            conv_tile(PRE, XC, lambda q: pc(l, "conv_c_w", q * 2 + j), pc(l, "conv_c_b", j))
            for d in range(2):
                for bi_, (t0, n) in enumerate(_blocks()):
                    b4, b5 = (4, 5) if bi_ % 2 == 0 else (6, 7)
                    S.pe("matmul", out=ps(b4, 0, n), lhsT=WBD[:, d * 4 + j, :], rhs=XC[:, t0:t0 + n], start=True, stop=True)
                    S.pe("matmul", out=ps(b5, 0, n), lhsT=WBD[:, d * 4 + 2 + j, :], rhs=XC[:, t0:t0 + n], start=True, stop=True)
                    S.act("activation", out=RGf[:, t0:t0 + n], in_=ps(b4, 0, n), func=AF.Sigmoid, bias=pc(l, "b_rg", d * 4 + j))
                    S.act("activation", out=IGf[:, t0:t0 + n], in_=ps(b5, 0, n), func=AF.Sigmoid, bias=pc(l, "b_rg", d * 4 + 2 + j))
                S.act("activation", out=AA, in_=RGf, func=AF.Exp, scale=CL[:, d * 2 + j, 0:1])
                S.act("activation", out=RGf, in_=RGf, func=AF.Exp, scale=CL[:, d * 2 + j, 1:2])
                S.act("activation", out=RGf, in_=RGf, func=AF.Sqrt, scale=-1.0, bias=1.0)
                S.dve("tensor_tensor", out=IGf, in0=IGf, in1=RGf, op=ALU.mult)
                S.dve("tensor_tensor", out=UU, in0=IGf, in1=XC, op=ALU.mult)
                if d == 0:
                    S.dve("tensor_tensor_scan", out=HS, data0=AA, data1=UU, initial=0.0, op0=ALU.mult, op1=ALU.add)
                else:
                    S.dve("tensor_tensor_scan", out=H1[:, 0:TC][:, ::-1], data0=AA[:, 0:TC][:, ::-1],
                          data1=UU[:, 0:TC][:, ::-1], initial=0.0, op0=ALU.mult, op1=ALU.add)
                    S.dve("tensor_tensor_scan", out=H1[:, TC:T][:, ::-1], data0=AA[:, TC:T][:, ::-1],
                          data1=UU[:, TC:T][:, ::-1], initial=H1[:, 0:1], op0=ALU.mult, op1=ALU.add)
                    S.pool("tensor_tensor", out=HS, in0=HS, in1=H1, op=ALU.add)
            S.act("activation", out=AA, in_=GR, func=AF.Square)
            S.dve("tensor_scalar", out=AA, in0=AA, scalar1=0.044715, scalar2=1.0, op0=ALU.mult, op1=ALU.add)
            S.pool("tensor_tensor", out=AA, in0=AA, in1=GR, op=ALU.mult)
            S.act("activation", out=AA, in_=AA, func=AF.Sigmoid, scale=GELU_C)
            S.pool("tensor_tensor", out=GR, in0=AA, in1=GR, op=ALU.mult)
            S.dve("tensor_tensor", out=Y[:, 6 + j, 0:TC], in0=HS[:, 0:TC], in1=GR[:, 0:TC], op=ALU.mult)
            S.dve("tensor_tensor", out=Y[:, 6 + j, TC:T].rearrange("p (r w) -> p r w", w=64),
                  in0=HS[:, TC:T].rearrange("p (w r) -> p r w", r=32),
                  in1=GR[:, TC:T].rearrange("p (r w) -> p r w", w=64), op=ALU.mult)
        A.release(m)

    def gate_post(G, EL, R, n, d, GW, par=0):
        nch = n // 128
        EG, TW, DECX, TG, DECB = GW
        e = 127 if d == 0 else 0
        G3 = G[0:R, 0:n].rearrange("p (c l) -> p c l", l=128)
        EG3 = EG[0:R, 0:n].rearrange("p (c l) -> p c l", l=128)
        S.act("activation", out=EG[0:R, 0:n], in_=G[0:R, 0:n], func=AF.Exp)
        S.dve("tensor_tensor", out=TW[0:R, 0:n].rearrange("p (c l) -> p c l", l=128),
              in0=G3[:, :, e:e + 1].to_broadcast([R, nch, 128]), in1=G3, op=ALU.subtract)
        S.act("activation", out=TW[0:R, 0:n], in_=TW[0:R, 0:n], func=AF.Exp)
        S.dve("tensor_tensor", out=DECX[0:R, 0:R, 0:nch],
              in0=EG3[:, :, e].unsqueeze(1).to_broadcast([R, R, nch]),
              in1=IDENT[0:R, 0:R].unsqueeze(2).to_broadcast([R, R, nch]), op=ALU.mult)
        return TG[par], DECB[par]

    def gate_post2(EL, R, n, GW, par=0):
        nch = n // 128
        EG, TW, DECX, TG, DECB = GW
        TG = TG[par]
        DECB = DECB[par]
        S.pe("matmul", out=ps(0, 0, R * nch).rearrange("p (u c) -> p u c", c=nch), lhsT=CONST[0:R, C_GSEL:C_GSEL + 128],
             rhs=DECX[0:R, 0:R, 0:nch], start=True, stop=True)
        S.act("activation", out=DECB[:, 0:R, 0:nch], in_=ps(0, 0, R * nch).rearrange("p (u c) -> p u c", c=nch),
              func=AF.Copy)
        for ci in range(nch):
            for q, SRC in enumerate((EL, TW, EG)):
                o = (ci * 3 + q) * R
                S.pe("matmul", out=ps(1, o, o + R), lhsT=SRC[0:R, ci * 128:(ci + 1) * 128],
                     rhs=IDENT[0:R, 0:R], start=True, stop=True)
        S.dve("tensor_copy", out=TG[:, 0:nch, :, 0:R],
              in_=ps(1, 0, nch * 3 * R).rearrange("p (c q r) -> p c q r", q=3, r=R))

    def alloc_gate_ws():
        EL = A.alloc([8, 512], F32)
        T1 = A.alloc([8, 512], F32)
        G = A.alloc([8, 512], F32)
        EG = A.alloc([8, 512], F32)
        TW = A.alloc([8, 512], F32)
        DECX = A.alloc([8, 8, 4], F32)
        TG = [A.alloc([128, 4, 3, 8], F32) for _ in range(2)]
        DECB = [A.alloc([128, 8, 4], F32) for _ in range(2)]
        return EL, T1, G, (EG, TW, DECX, TG, DECB)

    def sweep_blocks(d):
        blks = _blocks()
        order = [0, 1, 2, 3, 4] if d == 0 else [0, 4, 3, 2, 1]
        for bi in order:
            t0, n = blks[bi]
            nch = n // 128
            cis = list(range(nch)) if d == 0 else list(range(nch - 1, -1, -1))
            yield t0, n, cis

    def gdiff(bank, j, u, GH, GL, c0, d):
        o = ps(bank, j * 128, (j + 1) * 128)
        sel = SELB[:, u * 128:(u + 1) * 128]
        nsel = SELB[:, 1024 + u * 128:1024 + (u + 1) * 128]
        S.pe("matmul", out=o, lhsT=sel, rhs=GH[:, c0:c0 + 128], start=True, stop=False)
        S.pe("matmul", out=o, lhsT=sel, rhs=GL[:, c0:c0 + 128], start=False, stop=False)
        S.pe("matmul", out=o, lhsT=GH[:, c0:c0 + 128], rhs=nsel, start=False, stop=False)
        S.pe("matmul", out=o, lhsT=GL[:, c0:c0 + 128], rhs=nsel, start=False, stop=False)
        S.pe("matmul", out=o, lhsT=IDB, rhs=MADD[:, d, :], start=False, stop=True)

    def split_hilo(G, GH, GL, R, n):
        S.act("activation", out=GH[0:R, 0:n], in_=G[0:R, 0:n], func=AF.Copy)
        S.dve("tensor_tensor", out=GL[0:R, 0:n], in0=G[0:R, 0:n], in1=GH[0:R, 0:n], op=ALU.subtract)

    def mixer_a(l):
        m = A.mark()
        QZ = A.alloc([128, 4, T], BF16)
        KT = A.alloc([128, 2, T], BF16)
        S.pool("memset", ap=QZ, constant=0.0)
        WVO = A.alloc([128, 8, 512], BF16)
        GWA = A.alloc([128, 8, 16], BF16)
        YB = A.alloc([128, 18, 256], BF16)
        load_w(WVO, "wvo", w_in[l], 512, 512)
        load_w(GWA, "gwa", w_in[l], 1024, 16)
        m2 = A.mark()
        PRE = A.alloc([128, 2312], F32)
        CT = A.alloc([128, T], F32)
        WS = [A.alloc([128, 8, 128], BF16) for _ in range(2)]
        zero_halo(PRE)
        for ti in range(4):
            load_w(WS[ti % 2], "wc%d" % (ti % 2), w_in[l], 128 * ti, 128)
            for (t0, n) in _blocks():
                proj_fm(ps(2 + (t0 // 512) % 2, 0, n), WS[ti % 2], 128, t0, n)
                dst = PRE[:, 2 + t0:2 + t0 + n] if t0 < TC else PRE[:, 261 + t0 - TC:261 + t0 - TC + n]
                S.act("activation", out=dst, in_=ps(2 + (t0 // 512) % 2, 0, n), func=AF.Copy)
            conv_tile(PRE, CT, lambda q: pc(l, "conv_a_w", q * 4 + ti), pc(l, "conv_a_b", ti))
            if ti < 2:
                S.act("activation", out=QZ[0:64, 2 * ti, :], in_=CT[0:64, :], func=AF.Silu)
                S.act("activation", out=QZ[64:128, 2 * ti + 1, :], in_=CT[64:128, :], func=AF.Silu)
            else:
                S.act("activation", out=KT[:, ti - 2, :], in_=CT, func=AF.Silu)
        A.release(m2)
        EL, T1, G, GW = alloc_gate_ws()
        GHs = [A.alloc([128, 512], BF16) for _ in range(2)]
        GLs = [A.alloc([128, 512], BF16) for _ in range(2)]
        for _g in GHs + GLs:
            S.pool("memset", ap=_g, constant=0.0)
        CS = A.alloc([128, 4, 65], F32)
        CB = A.alloc([128, 4, 65], BF16)
        BLK = A.alloc([128, 4, 65], F32)
        S.pool("memset", ap=BLK, constant=0.0)
        for h in range(4):
            S.pool("memset", ap=BLK[64 * (h % 2):64 * (h % 2) + 64, h, :], constant=1.0)
        VA = A.alloc([128, 4, 65], BF16)
        S.pool("memset", ap=VA, constant=1.0)
        OG = [A.alloc([128, 256], F32) for _ in range(3)]
        IC = [A.alloc([128, 4, 65], F32) for _ in range(3)]
        UC = [A.alloc([128, 4, 65], F32) for _ in range(2)]
        KTMs = [A.alloc([128, 256], BF16) for _ in range(2)]
        VHs = [A.alloc([128, 4, 65], BF16) for _ in range(2)]
        VTs = [A.alloc([128, 4, 65], BF16) for _ in range(2)]
        DTts = [A.alloc([128, 512], BF16) for _ in range(2)]
        STms = [A.alloc([128, 4, 128], BF16) for _ in range(2)]
        SSBs = [A.alloc([128, 4, 128], BF16) for _ in range(2)]
        YTs = [A.alloc([128, 4, 65], F32) for _ in range(2)]
        DN = A.alloc([128, 4, 1], F32)
        HV = A.alloc([128, 4, 64], F32)
        HQ = A.alloc([128, 4, 64], F32)
        SSq = A.alloc([128, 4], F32)
        YA = [A.alloc([128, 256], BF16) for _ in range(2)]

        def gate_prep(t0, n, d, par):
            proj_fm(ps(0, 0, n, p=4), GWA[:, :, 4 * d:4 * d + 4], 4, t0, n)
            proj_fm(ps(1, 0, n, p=4), GWA[:, :, 8 + 4 * d:8 + 4 * d + 4], 4, t0, n)
            S.act("activation", out=EL[0:4, 0:n], in_=ps(0, 0, n, p=4), func=AF.Exp, bias=GPX[0:4, d, 0:1])
            S.act("activation", out=T1[0:4, 0:n], in_=ps(1, 0, n, p=4), func=AF.Exp, scale=-1.0, bias=GPX[0:4, d, 1:2])
            S.act("activation", out=T1[0:4, 0:n], in_=T1[0:4, 0:n], func=AF.Ln, bias=1.0)
            if d == 0:
                S.dve("tensor_tensor_scan", out=G[0:4, 0:n], data0=SMASK[0:4, 0, 0:n], data1=T1[0:4, 0:n],
                      initial=0.0, op0=ALU.mult, op1=ALU.subtract)
            else:
                S.dve("tensor_tensor_scan", out=G[0:4, 0:n][:, ::-1], data0=SMASK[0:4, 1, 0:n][:, ::-1],
                      data1=T1[0:4, 0:n][:, ::-1], initial=0.0, op0=ALU.mult, op1=ALU.subtract)
            split_hilo(G, GHs[par], GLs[par], 4, n)
            TG, DECB = gate_post(G, EL, 4, n, d, GW, par)
            return [TG, DECB, GHs[par], GLs[par], (lambda: gate_post2(EL, 4, n, GW, par))]

        def front(ch, d, final):
            tc, ci, TG, DECB, par, GH, GL, par3 = ch
            KTM, VH, VT, DTt, STm = KTMs[par], VHs[par], VTs[par], DTts[par], STms[par]
            ncol = 512 if final else 256
            proj_tm(ps(2, 0, ncol), WVO, 0, ncol, tc)
            S.act("activation", out=VA[:, :, 0:64], in_=ps(2, 0, 256).rearrange("p (h e) -> p h e", e=64), func=AF.Copy)
            if final:
                S.act("activation", out=OG[par3], in_=ps(2, 256, 512), func=AF.Exp, scale=-1.0)
                S.act("activation", out=OG[par3], in_=OG[par3], func=AF.Ln, bias=1.0)
                S.act("activation", out=OG[par3], in_=OG[par3], func=AF.Exp, scale=-1.0)
            for u in range(4):
                gdiff(5, u, u, GH, GL, ci * 128, d)
            S.act("activation", out=DTt, in_=ps(5), func=AF.Exp)
            for h in range(4):
                S.pe("matmul", out=ps(4, h * 128, (h + 1) * 128), lhsT=KT[:, h // 2, tc:tc + 128],
                     rhs=QZ[:, h, tc:tc + 128], start=True, stop=True)
            S.dve("tensor_tensor", out=STm, in0=DTt.rearrange("p (u t) -> p u t", t=128),
                  in1=ps(4).rearrange("p (u t) -> p u t", t=128), op=ALU.mult)
            for tt in range(2):
                S.pe("transpose", out=ps_bf(3)[:, tt * 128:(tt + 1) * 128], in_=KT[:, tt, tc:tc + 128], identity=IDB)
            S.dve("tensor_copy", out=KTM, in_=ps_bf(3)[:, 0:256])
            S.pool("tensor_tensor", out=VH, in0=VA, in1=TG[:, ci, 0, 0:4].unsqueeze(2).to_broadcast([128, 4, 65]), op=ALU.mult)
            S.pool("tensor_tensor", out=VT, in0=VH, in1=TG[:, ci, 1, 0:4].unsqueeze(2).to_broadcast([128, 4, 65]), op=ALU.mult)
            for tt in range(2):
                S.pe("matmul", out=ps(0, tt * 130, tt * 130 + 130), lhsT=KTM[:, tt * 128:(tt + 1) * 128],
                     rhs=VT[:, 2 * tt:2 * tt + 2, :], start=True, stop=True)
            S.dve("tensor_tensor", out=UC[par], in0=ps(0, 0, 260).rearrange("p (u v) -> p u v", v=65), in1=BLK, op=ALU.mult)
            for u in range(4):
                S.pe("matmul", out=ps(6, u * 65, u * 65 + 65), lhsT=STm[:, u, :], rhs=VH[:, u, :], start=True, stop=True)
            S.act("activation", out=IC[par3], in_=ps(6, 0, 260).rearrange("p (u v) -> p u v", v=65), func=AF.Copy)

        def back(ch, d, final):
            tc, ci, TG, DECB, par, GH, GL, par3 = ch
            YT = YTs[par]
            for h in range(4):
                S.pe("matmul", out=ps(7, h * 65, h * 65 + 65), lhsT=QZ[:, h, tc:tc + 128], rhs=CB[:, h, :], start=True, stop=True)
            S.dve("tensor_tensor", out=YT, in0=ps(7, 0, 260).rearrange("p (u v) -> p u v", v=65),
                  in1=TG[:, ci, 2, 0:4].unsqueeze(2).to_broadcast([128, 4, 65]), op=ALU.mult)
            S.pool("tensor_tensor", out=CS, in0=CS, in1=DECB[:, 0:4, ci:ci + 1].to_broadcast([128, 4, 65]), op=ALU.mult)
            S.pool("tensor_tensor", out=CS, in0=CS, in1=UC[par], op=ALU.add)
            S.act("activation", out=CB, in_=CS, func=AF.Copy)

        def epi(ch, d, final):
            tc, ci, TG, DECB, par, GH, GL, par3 = ch
            c = tc // 128
            YT = YTs[par]
            S.dve("tensor_tensor", out=YT, in0=YT, in1=IC[par3], op=ALU.add)
            S.dve("tensor_scalar", out=DN, in0=YT[:, :, 64:65], scalar1=-1.0, scalar2=1.0, op0=ALU.mult, op1=ALU.max)
            S.dve("tensor_tensor", out=DN, in0=DN, in1=YT[:, :, 64:65], op=ALU.max)
            S.dve("reciprocal", out=DN, in_=DN)
            if not final:
                S.dve("tensor_tensor", out=YB[:, c, :].rearrange("p (u v) -> p u v", v=64), in0=YT[:, :, 0:64],
                      in1=DN.to_broadcast([128, 4, 64]), op=ALU.mult)
            else:
                S.dve("tensor_tensor", out=HV, in0=YT[:, :, 0:64], in1=DN.to_broadcast([128, 4, 64]), op=ALU.mult)
                S.dve("tensor_tensor", out=HV, in0=HV, in1=YB[:, c, :].rearrange("p (u v) -> p u v", v=64), op=ALU.add)
                S.dve("tensor_tensor", out=HQ, in0=HV, in1=HV, op=ALU.mult)
                S.dve("tensor_reduce", out=SSq, in_=HQ, axis=AX.X, op=ALU.add)
                S.act("activation", out=SSq, in_=SSq, func=AF.Ln, scale=1.0 / 64, bias=EPS)
                S.act("activation", out=SSq, in_=SSq, func=AF.Exp, scale=-0.5)
                S.dve("tensor_tensor", out=HV, in0=HV, in1=SSq.unsqueeze(2).to_broadcast([128, 4, 64]), op=ALU.mult)
                S.dve("tensor_tensor", out=HV, in0=HV, in1=ROWP[:, 0:256].rearrange("p (u v) -> p u v", v=64), op=ALU.mult)
                S.dve("tensor_tensor", out=YA[par].rearrange("p (u v) -> p u v", v=64), in0=HV,
                      in1=OG[par3].rearrange("p (u v) -> p u v", v=64), op=ALU.mult)

        def back2(ch, d, final):
            tc, ci, TG, DECB, par, GH, GL, par3 = ch
            if final:
                for tt in range(2):
                    S.pe("transpose", out=ps_bf(1)[:, 512 + tt * 128:512 + (tt + 1) * 128],
                         in_=YA[par][:, tt * 128:(tt + 1) * 128], identity=IDB)
                S.act("activation", out=Y[:, 0:2, tc:tc + 128],
                      in_=ps_bf(1)[:, 512:768].rearrange("p (a b) -> p a b", a=2), func=AF.Copy)

        for d in (1, 0):
            final = d == 0
            S.pool("memset", ap=CS, constant=0.0)
            S.pool("memset", ap=CB, constant=0.0)
            blist = list(sweep_blocks(d))
            gates = {0: gate_prep(blist[0][0], blist[0][1], d, 0)}
            gates[0][4]()
            q = [None, None, None]
            cnt = 0

            def advance(newch):
                if q[0] is not None:
                    back(q[0], d, final)
                if q[1] is not None:
                    epi(q[1], d, final)
                if q[2] is not None:
                    back2(q[2], d, final)
                q[2] = q[1]
                q[1] = q[0]
                q[0] = newch

            for bidx, (t0, n, cis) in enumerate(blist):
                TG, DECB, GH, GL, _p2 = gates.pop(bidx)
                for k_, ci in enumerate(cis):
                    ch = (t0 + ci * 128, ci, TG, DECB, cnt % 2, GH, GL, cnt % 3)
                    cnt += 1
                    front(ch, d, final)
                    if q[0] is not None:
                        back(q[0], d, final)
                    if k_ == 0 and bidx + 1 < len(blist):
                        gates[bidx + 1] = gate_prep(blist[bidx + 1][0], blist[bidx + 1][1], d, (bidx + 1) % 2)
                    if k_ == 1 and bidx + 1 < len(blist):
                        gates[bidx + 1][4]()
                    if q[1] is not None:
                        epi(q[1], d, final)
                    if q[2] is not None:
                        back2(q[2], d, final)
                    q[2] = q[1]
                    q[1] = q[0]
                    q[0] = ch
            for _ in range(3):
                advance(None)
        A.release(m)

    def mixer_b(l):
        m = A.mark()
        XT4 = A.alloc([128, 4, T], BF16)
        BT = A.alloc([128, T], BF16)
        CZ = A.alloc([128, 2, T], BF16)
        WZ = A.alloc([128, 8, 512], BF16)
        GWB = A.alloc([128, 8, 16], BF16)
        YB = A.alloc([128, 18, 512], BF16)
        S.pool("memset", ap=CZ, constant=0.0)
        load_w(WZ, "wz", w_in[l], 1040, 512)
        load_w(GWB, "gwb", w_in[l], 2320, 16)
        m2 = A.mark()
        PRE = A.alloc([128, 2312], F32)
        CT = A.alloc([128, T], F32)
        WS = [A.alloc([128, 8, 128], BF16) for _ in range(2)]
        zero_halo(PRE)
        for ti in range(6):
            c0 = 1552 + 128 * ti if ti < 4 else (2064 if ti == 4 else 2192)
            load_w(WS[ti % 2], "wc%d" % (ti % 2), w_in[l], c0, 128)
            for (t0, n) in _blocks():
                proj_fm(ps(2 + (t0 // 512) % 2, 0, n), WS[ti % 2], 128, t0, n)
                dst = PRE[:, 2 + t0:2 + t0 + n] if t0 < TC else PRE[:, 261 + t0 - TC:261 + t0 - TC + n]
                S.act("activation", out=dst, in_=ps(2 + (t0 // 512) % 2, 0, n), func=AF.Copy)
            conv_tile(PRE, CT, lambda q: pc(l, "conv_b_w", q * 6 + ti), pc(l, "conv_b_b", ti))
            if ti < 4:
                S.act("activation", out=XT4[:, ti, :], in_=CT, func=AF.Silu)
            elif ti == 4:
                S.act("activation", out=BT, in_=CT, func=AF.Silu)
            else:
                S.act("activation", out=CZ[0:64, 0, :], in_=CT[0:64, :], func=AF.Silu)
                S.act("activation", out=CZ[64:128, 1, :], in_=CT[64:128, :], func=AF.Silu)
        A.release(m2)
        EL, T1, G, GW = alloc_gate_ws()
        GHs = [A.alloc([128, 512], BF16) for _ in range(2)]
        GLs = [A.alloc([128, 512], BF16) for _ in range(2)]
        for _g in GHs + GLs:
            S.pool("memset", ap=_g, constant=0.0)
        CS = A.alloc([128, 8, 64], F32)
        CB = A.alloc([128, 8, 64], BF16)
        BLK = A.alloc([128, 8, 64], F32)
        S.pool("memset", ap=BLK, constant=0.0)
        S.pool("memset", ap=BLK[0:64, 0:4, :], constant=1.0)
        S.pool("memset", ap=BLK[64:128, 4:8, :], constant=1.0)
        ZS = [A.alloc([128, 512], BF16) for _ in range(3)]
        ZT = A.alloc([128, 512], F32)
        XTM = [A.alloc([128, 8, 64], BF16) for _ in range(3)]
        IC = [A.alloc([128, 512], BF16) for _ in range(3)]
        UC = [A.alloc([128, 8, 64], F32) for _ in range(2)]
        BTMs = [A.alloc([128, 128], BF16) for _ in range(2)]
        VHs = [A.alloc([128, 8, 64], BF16) for _ in range(2)]
        VTs = [A.alloc([128, 8, 64], BF16) for _ in range(2)]
        DTts = [[A.alloc([128, 512], BF16) for _ in range(2)] for _ in range(2)]
        STms = [[A.alloc([128, 4, 128], BF16) for _ in range(2)] for _ in range(2)]
        SSBs = [A.alloc([128, 256], BF16) for _ in range(2)]
        ZC = A.alloc([128, 512], BF16)
        YTs = [A.alloc([128, 8, 64], F32) for _ in range(2)]
        YS = A.alloc([128, 512], F32)
        XD = A.alloc([128, 512], BF16)
        SS1 = A.alloc([128, 1], F32)
        YO = [A.alloc([128, 512], BF16) for _ in range(2)]

        def gate_prep(t0, n, d, par):
            proj_fm(ps(0, 0, n, p=8), GWB[:, :, 8 * d:8 * d + 8], 8, t0, n)
            S.act("activation", out=T1[0:8, 0:n], in_=ps(0, 0, n, p=8), func=AF.Exp, bias=GPX[0:8, d, 2:3])
            S.act("activation", out=EL[0:8, 0:n], in_=T1[0:8, 0:n], func=AF.Ln, bias=1.0)
            S.dve("tensor_scalar", out=T1[0:8, 0:n], in0=EL[0:8, 0:n], scalar1=GPX[0:8, d, 3:4], scalar2=None, op0=ALU.mult)
            if d == 0:
                S.dve("tensor_tensor_scan", out=G[0:8, 0:n], data0=SMASK[0:8, 0, 0:n], data1=T1[0:8, 0:n],
                      initial=0.0, op0=ALU.mult, op1=ALU.add)
            else:
                S.dve("tensor_tensor_scan", out=G[0:8, 0:n][:, ::-1], data0=SMASK[0:8, 1, 0:n][:, ::-1],
                      data1=T1[0:8, 0:n][:, ::-1], initial=0.0, op0=ALU.mult, op1=ALU.add)
            split_hilo(G, GHs[par], GLs[par], 8, n)
            TG, DECB = gate_post(G, EL, 8, n, d, GW, par)
            return [TG, DECB, GHs[par], GLs[par], (lambda: gate_post2(EL, 8, n, GW, par))]

        def front(ch, d, final):
            tc, ci, TG, DECB, par, GH, GL, par3 = ch
            BTM, VH, VT, DTt, STm = BTMs[par], VHs[par], VTs[par], DTts[par], STms[par]
            if final:
                proj_tm(ps(2), WZ, 0, 512, tc)
                S.act("activation", out=ZT, in_=ps(2), func=AF.Exp, scale=-1.0)
                S.act("activation", out=ZT, in_=ZT, func=AF.Ln, bias=1.0)
                S.act("activation", out=ZT, in_=ZT, func=AF.Exp, scale=-1.0)
                S.dve("tensor_tensor", out=ZS[par3], in0=ZT, in1=ps(2), op=ALU.mult)
            for i in range(4):
                S.pe("transpose", out=ps_bf(3)[:, i * 128:(i + 1) * 128], in_=XT4[:, i, tc:tc + 128], identity=IDB)
            S.pe("transpose", out=ps_bf(3)[:, 512:640], in_=BT[:, tc:tc + 128], identity=IDB)
            xtm = XTM[par3]
            S.dve("tensor_copy", out=xtm.rearrange("p a b -> p (a b)"), in_=ps_bf(3)[:, 0:512])
            S.dve("tensor_copy", out=BTM, in_=ps_bf(3)[:, 512:640])
            S.pool("tensor_tensor", out=VH, in0=xtm, in1=TG[:, ci, 0, 0:8].unsqueeze(2).to_broadcast([128, 8, 64]), op=ALU.mult)
            S.pool("tensor_tensor", out=VT, in0=VH, in1=TG[:, ci, 1, 0:8].unsqueeze(2).to_broadcast([128, 8, 64]), op=ALU.mult)
            for g in range(2):
                bank = 5 if g == 0 else 4
                for uu in range(4):
                    gdiff(bank, uu, 4 * g + uu, GH, GL, ci * 128, d)
                S.act("activation", out=DTt[g], in_=ps(bank), func=AF.Exp)
            for g in range(2):
                S.pe("matmul", out=ps(1, g * 128, (g + 1) * 128), lhsT=BT[:, tc:tc + 128], rhs=CZ[:, g, tc:tc + 128],
                     start=True, stop=True)
            for g in range(2):
                S.dve("tensor_tensor", out=STm[g], in0=DTt[g].rearrange("p (u t) -> p u t", t=128),
                      in1=ps(1, g * 128, (g + 1) * 128).unsqueeze(1).to_broadcast([128, 4, 128]), op=ALU.mult)
            S.pe("matmul", out=ps(0), lhsT=BTM, rhs=VT.rearrange("p a b -> p (a b)"), start=True, stop=True)
            S.dve("tensor_tensor", out=UC[par], in0=ps(0).rearrange("p (a b) -> p a b", b=64), in1=BLK, op=ALU.mult)
            for e in range(8):
                S.pe("matmul", out=ps(6, e * 64, e * 64 + 64), lhsT=STm[e // 4][:, e % 4, :], rhs=VH[:, e, :],
                     start=True, stop=True)
            S.act("activation", out=IC[par3], in_=ps(6), func=AF.Copy)

        def back(ch, d, final):
            tc, ci, TG, DECB, par, GH, GL, par3 = ch
            YT = YTs[par]
            for g in range(2):
                S.pe("matmul", out=ps(7, g * 256, (g + 1) * 256), lhsT=CZ[:, g, tc:tc + 128],
                     rhs=CB[:, 4 * g:4 * g + 4, :], start=True, stop=True)
            S.dve("tensor_tensor", out=YT, in0=ps(7).rearrange("p (u v) -> p u v", v=64),
                  in1=TG[:, ci, 2, 0:8].unsqueeze(2).to_broadcast([128, 8, 64]), op=ALU.mult)
            S.pool("tensor_tensor", out=CS, in0=CS, in1=DECB[:, 0:8, ci:ci + 1].to_broadcast([128, 8, 64]), op=ALU.mult)
            S.pool("tensor_tensor", out=CS, in0=CS, in1=UC[par], op=ALU.add)
            S.act("activation", out=CB, in_=CS, func=AF.Copy)

        def epi(ch, d, final):
            tc, ci, TG, DECB, par, GH, GL, par3 = ch
            c = tc // 128
            ytf = YTs[par].rearrange("p a b -> p (a b)")
            if not final:
                S.dve("tensor_tensor", out=YB[:, c, :], in0=ytf, in1=IC[par3], op=ALU.add)
            else:
                S.dve("tensor_tensor", out=YS, in0=ytf, in1=IC[par3], op=ALU.add)
                S.dve("tensor_tensor", out=YS, in0=YS, in1=YB[:, c, :], op=ALU.add)
                S.dve("tensor_tensor", out=XD, in0=XTM[par3].rearrange("p a b -> p (a b)"), in1=ROWP[:, 768:1280], op=ALU.mult)
                S.dve("tensor_tensor", out=YS, in0=YS, in1=XD, op=ALU.add)
                S.dve("tensor_tensor", out=YS, in0=YS, in1=ZS[par3], op=ALU.mult)
                S.act("activation", out=XD, in_=YS, func=AF.Square, accum_out=SS1)
                S.act("activation", out=SS1, in_=SS1, func=AF.Ln, scale=1.0 / 512, bias=EPS)
                S.act("activation", out=SS1, in_=SS1, func=AF.Exp, scale=-0.5)
                S.dve("scalar_tensor_tensor", out=YO[par], in0=YS, scalar=SS1[:, 0:1], in1=ROWP[:, 256:768],
                      op0=ALU.mult, op1=ALU.mult)

        def back2(ch, d, final):
            tc, ci, TG, DECB, par, GH, GL, par3 = ch
            if final:
                for i in range(4):
                    S.pe("transpose", out=ps_bf(1)[:, 512 + i * 128:512 + (i + 1) * 128], in_=YO[par][:, i * 128:(i + 1) * 128],
                         identity=IDB)
                S.act("activation", out=Y[:, 2:6, tc:tc + 128],
                      in_=ps_bf(1)[:, 512:1024].rearrange("p (a b) -> p a b", a=4), func=AF.Copy)

        for d in (1, 0):
            final = d == 0
            chk("b_d%d" % d)
            S.pool("memset", ap=CS, constant=0.0)
            S.pool("memset", ap=CB, constant=0.0)
            blist = list(sweep_blocks(d))
            gates = {0: gate_prep(blist[0][0], blist[0][1], d, 0)}
            gates[0][4]()
            q = [None, None, None]
            cnt = 0

            def advance(newch):
                if q[0] is not None:
                    back(q[0], d, final)
                if q[1] is not None:
                    epi(q[1], d, final)
                if q[2] is not None:
                    back2(q[2], d, final)
                q[2] = q[1]
                q[1] = q[0]
                q[0] = newch

            for bidx, (t0, n, cis) in enumerate(blist):
                TG, DECB, GH, GL, _p2 = gates.pop(bidx)
                for k_, ci in enumerate(cis):
                    ch = (t0 + ci * 128, ci, TG, DECB, cnt % 2, GH, GL, cnt % 3)
                    cnt += 1
                    front(ch, d, final)
                    if q[0] is not None:
                        back(q[0], d, final)
                    if k_ == 0 and bidx + 1 < len(blist):
                        gates[bidx + 1] = gate_prep(blist[bidx + 1][0], blist[bidx + 1][1], d, (bidx + 1) % 2)
                    if k_ == 1 and bidx + 1 < len(blist):
                        gates[bidx + 1][4]()
                    if q[1] is not None:
                        epi(q[1], d, final)
                    if q[2] is not None:
                        back2(q[2], d, final)
                    q[2] = q[1]
                    q[1] = q[0]
                    q[0] = ch
            for _ in range(3):
                advance(None)
        A.release(m)

    def phase_out_mlp(l, last):
        m = A.mark()
        XR = A.alloc([128, 8, T], F32)
        RS = A.alloc([128, 512], F32)
        TMP = [A.alloc([128, 512], F32) for _ in range(2)]
        ZB = [A.alloc([128, 512], F32) for _ in range(2)]
        WO = [A.alloc([128, 8, 128], BF16) for _ in range(2)]
        W1 = [A.alloc([128, 8, 128], BF16) for _ in range(3)]
        W2 = [A.alloc([128, 4, D], BF16) for _ in range(2)]
        HS_ = [Y[:, 0:4, :], Y[:, 4:8, :]]
        SQ = Y[:, 0:2, :].rearrange("p a t -> p (a t)")[:, 0:4096].rearrange("p (k n) -> p k n", k=8)
        src = xsrc(l)
        blks = [(t0, n) for (t0, n) in _blocks() if not (last and t0 < TC)]
        for bi, (t0, n) in enumerate(blks):
            S.dma("sp", "xs%d" % (bi % 2), out=XR[:, :, t0:t0 + n], in_=src[:, t0:t0 + n].rearrange("(k p) t -> p k t", p=128))
        chk("m_load")
        cnt = 0
        for i in range(8):
            wo = WO[i % 2]
            load_w(wo, "wo%d" % (i % 2), w_out[l], 128 * i, 128)
            for (t0, n) in blks:
                j = 1 if t0 < TC else 0
                bank = cnt % 4
                cnt += 1
                for k in range(8):
                    S.pe("matmul", out=ps(bank, 0, n), lhsT=wo[:, k, :], rhs=Y[:, k, t0:t0 + n], start=(k == 0), stop=(k == 7))
                S.dve("scalar_tensor_tensor", out=XR[:, i, t0:t0 + n], in0=ps(bank, 0, n), scalar=MOD[:, 16 + i, j:j + 1],
                      in1=XR[:, i, t0:t0 + n], op0=ALU.mult, op1=ALU.add)
        chk("m_wout")
        for (t0, n) in blks:
            j = 1 if t0 < TC else 0
            S.act("activation", out=SQ[:, :, 0:n], in_=XR[:, :, t0:t0 + n], func=AF.Square)
            for k in range(8):
                S.pe("matmul", out=ps(4, 0, n), lhsT=ONESB, rhs=SQ[:, k, 0:n], start=(k == 0), stop=(k == 7))
            S.act("activation", out=RS[:, 0:n], in_=ps(4, 0, n), func=AF.Ln, scale=1.0 / D, bias=EPS)
            S.act("activation", out=RS[:, 0:n], in_=RS[:, 0:n], func=AF.Exp, scale=-0.5)
            for k in range(8):
                tmp = TMP[k % 2]
                S.dve("tensor_tensor", out=tmp[:, 0:n], in0=XR[:, k, t0:t0 + n], in1=RS[:, 0:n], op=ALU.mult)
                S.act("activation", out=XN[:, k, t0:t0 + n], in_=tmp[:, 0:n], func=AF.Identity,
                      scale=A2[:, k, j:j + 1], bias=MOD[:, 24 + k, j:j + 1])
        chk("m_norm")
        c1 = 0
        c2 = 0
        for hg in range(8):
            Hg = HS_[hg % 2]
            w2 = W2[hg % 2]
            for hf in range(2):
                S.dma("pool", "w2%d%d" % (hg % 2, hf), out=w2[:, :, 512 * hf:512 * hf + 512],
                      in_=w_mlp2[l][512 * hg:512 * hg + 512, 512 * hf:512 * hf + 512].rearrange("(j p) n -> p j n", p=128))
            for jj in range(4):
                jh = 4 * hg + jj
                w1 = W1[jh % 3]
                load_w(w1, "w1%d" % (jh % 3), w_mlp1[l], 128 * jh, 128)
                for (t0, n) in blks:
                    bank = c1 % 4
                    c1 += 1
                    for k in range(8):
                        S.pe("matmul", out=ps(bank, 0, n), lhsT=w1[:, k, :], rhs=XN[:, k, t0:t0 + n], start=(k == 0), stop=(k == 7))
                    zb = ZB[c1 % 2]
                    S.dve("tensor_scalar", out=zb[:, 0:n], in0=ps(bank, 0, n), scalar1=pc(l, "b_mlp1", jh), scalar2=0.0,
                          op0=ALU.add, op1=ALU.max)
                    S.act("activation", out=Hg[:, jj, t0:t0 + n], in_=zb[:, 0:n], func=AF.Square)
            chk("m_h%d" % hg)
            for i in range(8):
                for (t0, n) in blks:
                    j = 1 if t0 < TC else 0
                    bank = 4 + c2 % 4
                    c2 += 1
                    for jj in range(4):
                        S.pe("matmul", out=ps(bank, 0, n), lhsT=w2[:, jj, 128 * i:128 * i + 128], rhs=Hg[:, jj, t0:t0 + n],
                             start=(jj == 0), stop=(jj == 3))
                    S.dve("scalar_tensor_tensor", out=XR[:, i, t0:t0 + n], in0=ps(bank, 0, n), scalar=MOD[:, 40 + i, j:j + 1],
                          in1=XR[:, i, t0:t0 + n], op0=ALU.mult, op1=ALU.add)
        chk("m_mlp")
        for i in range(8):
            for (ta, tb, j) in ((0, TC, 1), (TC, T, 0)):
                if last and j == 1:
                    continue
                S.act("activation", out=XR[:, i, ta:tb], in_=XR[:, i, ta:tb], func=AF.Identity, bias=GB2[:, i, j:j + 1])
        chk("m_b2")
        if not last:
            for bi, (t0, n) in enumerate(blks):
                S.dma("sp", "xo%d" % bi, out=xscr[:, t0:t0 + n].rearrange("(k p) t -> p k t", p=128), in_=XR[:, :, t0:t0 + n])
        else:
            for bi, (t0, n) in enumerate(blks):
                S.act("activation", out=SQ[:, :, 0:n], in_=XR[:, :, t0:t0 + n], func=AF.Square)
                for k in range(8):
                    S.pe("matmul", out=ps(4, 0, n), lhsT=ONESB, rhs=SQ[:, k, 0:n], start=(k == 0), stop=(k == 7))
                S.act("activation", out=RS[:, 0:n], in_=ps(4, 0, n), func=AF.Ln, scale=1.0 / D, bias=EPS)
                S.act("activation", out=RS[:, 0:n], in_=RS[:, 0:n], func=AF.Exp, scale=-0.5)
                for k in range(8):
                    S.dve("scalar_tensor_tensor", out=XR[:, k, t0:t0 + n], in0=XR[:, k, t0:t0 + n],
                          scalar=PC[:, PC_GFINAL + k:PC_GFINAL + k + 1], in1=RS[:, 0:n], op0=ALU.mult, op1=ALU.mult)
                chk("m_fin%d" % bi)
                ch = "xo%d" % bi
                S.dma("sp", ch, out=outT[:, t0 - TC:t0 - TC + n].rearrange("(k p) t -> p k t", p=128), in_=XR[:, :, t0:t0 + n])
                if ch not in final_chans:
                    final_chans.append(ch)
        A.release(m)

    try:
      for l in range(nlayers):
        _mm = A.mark()
        S.tag = "mod%d" % l
        phase_mod(l)
        A.release(_mm)
        if upto == "mod":
            break
        S.tag = "norm%d" % l
        phase_norm(l, xsrc(l), 0, A1)
        if upto == "norm1":
            break
        S.tag = "mixc%d" % l
        if "noc" not in taps:
            mixer_c(l)
        if upto == "mixc":
            break
        S.tag = "mixa%d" % l
        if "noa" not in taps:
            mixer_a(l)
        if upto == "mixa":
            break
        S.tag = "mixb%d" % l
        mixer_b(l)
        if upto == "mixb":
            break
        S.tag = "mlp%d" % l
        phase_out_mlp(l, l == nlayers - 1)
    except _Stop:
        pass
    tap("MOD", MOD)
    tap("XN", XN)
    tap("Y", Y)
    if "XS" in taps:
        pass
    if not final_chans:
        raise RuntimeError("no outputs")
    import os as _os
    if _os.environ.get("KTAGMAP"):
        S.tagmap = {}
    S.emit(stack, final_chans)
    if S.tagmap is not None:
        import json as _json
        _json.dump(S.tagmap, open(_os.environ["KTAGMAP"], "w"))
    stack.close()
    return nc, tap_out


def _pack_inputs(inputs):
    f = lambda a: np.ascontiguousarray(np.asarray(a, dtype=np.float32))
    pcols = np.zeros((NPC, 128), np.float32)
    for l in range(DEPTH):
        b = l * PC_PER_LAYER

        def put(name, arr):
            arr = f(arr).reshape(-1, 128)
            o = b + _PC_OFF[name]
            pcols[o:o + arr.shape[0]] = arr

        put("g_mix", inputs["g_mix"][l])
        put("g_mlp", inputs["g_mlp"][l])
        put("b_ada", inputs["b_ada"][l])
        put("b_mlp1", inputs["b_mlp1"][l])
        put("b_mlp2", inputs["b_mlp2"][l])
        put("conv_a_w", inputs["conv_a_w"][l])
        put("conv_a_b", inputs["conv_a_b"][l])
        put("conv_b_w", inputs["conv_b_w"][l])
        put("conv_b_b", inputs["conv_b_b"][l])
        put("conv_c_w", inputs["conv_c_w"][l])
        put("conv_c_b", inputs["conv_c_b"][l])
        put("b_rg", inputs["b_rg"][l])
        put("lam", inputs["lam"][l])
    pcols[PC_GFINAL:PC_GFINAL + 8] = f(inputs["g_final"]).reshape(8, 128)
    consts = np.zeros((128, NCONST), np.float32)
    consts[:, C_ID:C_ID + 128] = np.eye(128, dtype=np.float32)
    s = np.arange(128)[:, None]
    t = np.arange(128)[None, :]
    consts[:, C_MF:C_MF + 128] = np.where(s <= t, 0.0, -30000.0)
    consts[:, C_MB:C_MB + 128] = np.where(s >= t, 0.0, -30000.0)
    for u in range(8):
        consts[u, C_SEL + u * 128:C_SEL + (u + 1) * 128] = 1.0
        consts[u, C_NSEL + u * 128:C_NSEL + (u + 1) * 128] = -1.0
    consts[0:8, C_GSEL:C_GSEL + 128] = 1.0
    gpar = np.zeros((DEPTH, 2, 8, 4), np.float32)
    gpar[:, :, 0:4, 0] = f(inputs["b_ig"])
    gpar[:, :, 0:4, 1] = f(inputs["b_fg"])
    gpar[:, :, :, 2] = f(inputs["dt_bias"])
    gpar[:, :, :, 3] = f(inputs["a_log"])
    rowp = np.zeros((DEPTH, 1280), np.float32)
    rowp[:, 0:256] = f(inputs["g_head_a"]).reshape(DEPTH, 256)
    rowp[:, 256:768] = f(inputs["g_norm_b"])
    rowp[:, 768:1280] = np.repeat(f(inputs["d_skip"]), 64, axis=1)
    shared = {"pcols": pcols, "consts": consts, "gpar": gpar, "rowp": rowp,
              "w_ada": f(inputs["w_ada"]), "w_in": f(inputs["w_in"]), "w_out": f(inputs["w_out"]),
              "w_mlp1": f(inputs["w_mlp1"]), "w_mlp2": f(inputs["w_mlp2"]), "w_rg": f(inputs["w_rg"])}
    x = f(inputs["x"])
    ctx = f(inputs["ctx"])
    c = f(inputs["c"])
    c_ctx = f(inputs["c_ctx"])
    maps = []
    for b in range(x.shape[0]):
        xT = np.ascontiguousarray(np.concatenate([ctx[b], x[b]], axis=0).T)
        cv = np.stack([c[b], c_ctx], axis=0)
        cvT = np.ascontiguousarray(cv.reshape(2, 8, 128).transpose(2, 1, 0))
        mp = dict(shared)
        mp["xT"] = xT
        mp["cvT"] = cvT
        maps.append(mp)
    return maps


def kernel(**inputs):
    maps = _pack_inputs(inputs)
    nc, _ = build()
    res = run_bass_kernel_spmd(nc, maps, core_ids=list(range(8)))
    out = np.stack([np.ascontiguousarray(r["outT"].T) for r in res.results], axis=0)
    return out.astype(np.float32)
```
